# Optimizing a Trainium2 kernel written in Bass

```python
import jax, jax.numpy as jnp
from jax import lax
import numpy as np

D_MODEL = 1024
BATCH = 2
SEQ = 8192
DEPTH = 1
DEC_BATCH = 16
DEC_SEQ = 16
PAST_LEN = 1024

CHUNK = 64
MIX = D_MODEL
D_A = MIX // 2
D_B = MIX - D_A
H_A = 4
DK = D_A // H_A
DV = D_A // H_A
H_B = 4
C_B = D_B // H_B
MLP_CHUNK = 128
IN_WIDTH = 4 * D_A + 3 * D_B
EPS = 1e-6

kernel_name = "hgrn2_chunkmlp_hybrid_stream_step"


def rmsnorm(x, w):
    xf = x.astype(jnp.float32)
    y = xf * lax.rsqrt(jnp.mean(xf * xf, axis=-1, keepdims=True) + EPS)
    return (y * w.astype(jnp.float32)).astype(x.dtype)


def hgrn2_chunkwise(q, k, v, logf, s0, chunk):
    bsz, t, _ = q.shape
    n = t // chunk

    def heads(a, d):
        return a.reshape(bsz, n, chunk, H_A, d).transpose(1, 0, 3, 2, 4)

    qc, kc, gc, vc = heads(q, DK), heads(k, DK), heads(logf, DK), heads(v, DV)
    causal = jnp.tril(jnp.ones((chunk, chunk), dtype=bool))[:, :, None]

    def step(s, xs):
        qb, kb, vb, gb = xs
        b = jnp.cumsum(gb, axis=2)
        inter = jnp.einsum("bhtk,bhkv->bhtv", qb * jnp.exp(b), s)
        diff = b[:, :, :, None, :] - b[:, :, None, :, :]
        decay = jnp.exp(jnp.where(causal, diff, -jnp.inf))
        scores = jnp.einsum("bhtk,bhsk,bhtsk->bhts", qb, kb, decay)
        intra = jnp.einsum("bhts,bhsv->bhtv", scores, vb)
        b_last = b[:, :, -1, :]
        s_new = s * jnp.exp(b_last)[..., None] + jnp.einsum(
            "bhsk,bhsv->bhkv", kb * jnp.exp(b_last[:, :, None, :] - b), vb)
        return s_new, inter + intra

    s_fin, o = lax.scan(step, s0, (qc, kc, vc, gc))
    o = o.transpose(1, 0, 3, 2, 4).reshape(bsz, t, H_A * DV)
    return o, s_fin


def spatial_mix(v, w_s, b_s):
    bsz, t, _ = v.shape
    L = min(MLP_CHUNK, t)
    n = t // L
    w = jnp.tril(w_s[:, :L, :L].astype(jnp.float32))
    bias = b_s[:, :L].astype(jnp.float32).T
    vc = v.reshape(bsz, n, L, H_B, C_B)
    out = jnp.einsum("hts,bnshc->bnthc", w, vc) + bias[None, None, :, :, None]
    return out.reshape(bsz, t, D_B)


def mixer_layer(x, s0, norm_w, w_in, lb, g_norm_w, ln_v_w, ln_v_b, w_s, b_s, w_out):
    bsz, t, _ = x.shape
    h = rmsnorm(x, norm_w)
    z = jnp.einsum("btd,de->bte", h, w_in).astype(jnp.float32)
    q = z[..., 0:D_A]
    f_raw = z[..., D_A:2 * D_A]
    i = z[..., 2 * D_A:3 * D_A]
    g_a = z[..., 3 * D_A:4 * D_A]
    u = z[..., 4 * D_A:4 * D_A + D_B]
    v = z[..., 4 * D_A + D_B:4 * D_A + 2 * D_B]
    g_b = z[..., 4 * D_A + 2 * D_B:]

    f = lb + (1.0 - lb) * jax.nn.sigmoid(f_raw)
    o_a, s_fin = hgrn2_chunkwise(jax.nn.silu(q) * (DK ** -0.5), 1.0 - f, i, jnp.log(f),
                                 s0.astype(jnp.float32), min(CHUNK, t))
    o_a = o_a.reshape(bsz, t, H_A, DV)
    o_a = o_a * lax.rsqrt(jnp.mean(o_a * o_a, axis=-1, keepdims=True) + EPS)
    y_a = o_a.reshape(bsz, t, D_A) * g_norm_w.astype(jnp.float32) * jax.nn.silu(g_a)

    u = jax.nn.gelu(u, approximate=False)
    v = jax.nn.gelu(v, approximate=False)
    mu = jnp.mean(v, axis=-1, keepdims=True)
    var = jnp.mean(jnp.square(v - mu), axis=-1, keepdims=True)
    v = (v - mu) * lax.rsqrt(var + EPS) * ln_v_w.astype(jnp.float32) + ln_v_b.astype(jnp.float32)
    y_b = u * spatial_mix(v, w_s, b_s) * jax.nn.silu(g_b)

    y = jnp.concatenate([y_a, y_b], axis=-1).astype(x.dtype)
    out = x + jnp.einsum("bte,ed->btd", y, w_out).astype(x.dtype)
    return out, s_fin, v.astype(x.dtype)


def setup_inputs(seed: int = 0) -> dict:
    key = jax.random.key(seed)
    ks = jax.random.split(key, 14)
    f32 = jnp.float32
    return {
        "x_prompt": jax.random.normal(ks[0], (BATCH, SEQ, D_MODEL), f32),
        "x_sample": jax.random.normal(ks[1], (DEC_BATCH, DEC_SEQ, D_MODEL), f32),
        "state_hgrn": 0.5 * jax.random.normal(ks[2], (DEPTH, DEC_BATCH, H_A, DK, DV), f32),
        "norm_w": 1.0 + 0.02 * jax.random.normal(ks[3], (DEPTH, D_MODEL), f32),
        "w_in": jax.random.normal(ks[4], (DEPTH, D_MODEL, IN_WIDTH), f32) * D_MODEL ** -0.5,
        "lb_logits": 0.1 * jax.random.normal(ks[5], (DEPTH + 1, D_A), f32),
        "g_norm_w": 1.0 + 0.02 * jax.random.normal(ks[6], (DEPTH, D_A), f32),
        "ln_v_w": 1.0 + 0.02 * jax.random.normal(ks[7], (DEPTH, D_B), f32),
        "ln_v_b": 0.02 * jax.random.normal(ks[8], (DEPTH, D_B), f32),
        "w_s": jax.random.normal(ks[9], (DEPTH, H_B, MLP_CHUNK, MLP_CHUNK), f32) * MLP_CHUNK ** -0.5,
        "b_s": 1.0 + 0.1 * jax.random.normal(ks[10], (DEPTH, H_B, MLP_CHUNK), f32),
        "w_out": jax.random.normal(ks[11], (DEPTH, MIX, D_MODEL), f32) * MIX ** -0.5,
        "final_norm_w": 1.0 + 0.02 * jax.random.normal(ks[12], (D_MODEL,), f32),
    }


def reference(x_prompt, x_sample, state_hgrn, norm_w, w_in, lb_logits, g_norm_w, ln_v_w, ln_v_b,
              w_s, b_s, w_out, final_norm_w):
    lb_all = jnp.cumsum(jax.nn.softmax(lb_logits.astype(jnp.float32), axis=0), axis=0)
    hp, hs = x_prompt, x_sample
    sp_list, ss_list, v_list = [], [], []
    for l in range(DEPTH):
        params = (norm_w[l], w_in[l], lb_all[l], g_norm_w[l], ln_v_w[l], ln_v_b[l], w_s[l], b_s[l], w_out[l])
        s0_prompt = jnp.zeros((x_prompt.shape[0], H_A, DK, DV), jnp.float32)
        hp, s_p, _ = mixer_layer(hp, s0_prompt, *params)
        hs, s_s, v_s = mixer_layer(hs, state_hgrn[l], *params)
        sp_list.append(s_p)
        ss_list.append(s_s)
        v_list.append(v_s)
    y_prompt = rmsnorm(hp, final_norm_w)
    y_sample = rmsnorm(hs, final_norm_w)
    state_hgrn_prompt = jnp.stack(sp_list, axis=0)
    state_hgrn_sample = jnp.stack(ss_list, axis=0)
    mlp_v_sample = jnp.stack(v_list, axis=0)
    return (y_prompt, y_sample, state_hgrn_prompt, state_hgrn_sample, mlp_v_sample)
```

```python
import numpy as np
from contextlib import ExitStack

import concourse.bass as bass
import concourse.mybir as mybir
from concourse.bass_utils import run_bass_kernel_spmd

F32 = mybir.dt.float32
BF16 = mybir.dt.bfloat16
AF = mybir.ActivationFunctionType
ALU = mybir.AluOpType

NCORES = 8
D = 1024
KT = 8
INW = 3584
SEG = 2048
NTILE = SEG // 128
EPS = 1e-6
DK = 128
C_Q, C_F, C_I, C_GA, C_U, C_V, C_GB = 0, 512, 1024, 1536, 2048, 2560, 3072


class Buf:
    def __init__(self, name):
        self.name = name
        self.last_w = None
        self.readers = []


class Ins:
    __slots__ = ("idx", "eng", "emit", "deps", "waits", "sig", "clock", "is_dma", "key", "dur", "lat", "aset",
                 "final", "start", "finish", "nsucc", "succs", "npred", "fuse")


ACT_SWITCH_US = 1.3
ACT_SWITCH_DECISION_US = [1.3]
HOP_US = 1.0
FUSE_WAITS = [True]
VENG_FUSE = [1]


class Sched:
    ENGS = ("pe", "act", "dve", "pool", "sp")

    def __init__(self):
        self.ops = []
        self.final_waits = {}

    def op(self, eng, emit, reads=(), writes=(), dma_key=None, final=False, dur=0.3, lat=None, aset=None, fuse=0):
        ins = Ins()
        ins.fuse = fuse if FUSE_WAITS[0] else 0
        ins.idx = len(self.ops)
        ins.eng = eng
        ins.emit = emit
        ins.is_dma = dma_key is not None
        ins.key = dma_key if ins.is_dma else eng
        ins.dur = dur
        ins.lat = dur if lat is None else lat
        ins.aset = aset
        ins.final = final
        deps = {}
        for b in reads:
            if b.last_w is not None and b.last_w is not ins:
                deps[b.last_w.idx] = (b.last_w, "RAW")
        for b in writes:
            if b.last_w is not None and b.last_w is not ins:
                if b.last_w.idx not in deps:
                    deps[b.last_w.idx] = (b.last_w, "WAW")
            for r in b.readers:
                if r is not ins and r.idx not in deps:
                    deps[r.idx] = (r, "WAR")
        ins.deps = list(deps.values())
        for b in writes:
            b.last_w = ins
            b.readers = []
        for b in reads:
            if b.last_w is not ins:
                b.readers.append(ins)
        self.ops.append(ins)
        return ins

    def schedule(self, window=700):
        ops = self.ops
        n = len(ops)
        for o in ops:
            o.succs = []
            o.npred = len(o.deps)
            o.start = None
        for o in ops:
            for d, _ in o.deps:
                d.succs.append(o)
        free = {e: 0.0 for e in self.ENGS}
        cur_set = [None]
        avail = set(o.idx for o in ops if o.npred == 0)
        done = 0
        lo = 0
        order = []
        self.q = {e: [] for e in self.ENGS}
        while done < n:
            while lo < n and ops[lo].start is not None:
                lo += 1
            best = None
            best_t = None
            for i in avail:
                if i >= lo + window:
                    continue
                o = ops[i]
                rdy = 0.0
                for d, kind in o.deps:
                    f = d.finish
                    if d.eng != o.eng or d.is_dma or o.is_dma:
                        f += HOP_US
                    if f > rdy:
                        rdy = f
                t = max(rdy, free[o.eng])
                tdec = t
                if o.eng == "act" and o.aset is not None and cur_set[0] is not None and o.aset != cur_set[0]:
                    t += ACT_SWITCH_US
                    tdec = t - ACT_SWITCH_US + ACT_SWITCH_DECISION_US[0]
                if best is None or tdec < best_d - 1e-9 or (abs(tdec - best_d) <= 1e-9 and i < best.idx):
                    best, best_t, best_d = o, t, tdec
            o = best
            avail.discard(o.idx)
            if o.eng == "act" and o.aset is not None:
                cur_set[0] = o.aset
            o.start = best_t
            free[o.eng] = best_t + o.dur
            o.finish = best_t + o.lat
            order.append(o)
            self.q[o.eng].append(o)
            done += 1
            for s_ in o.succs:
                s_.npred -= 1
                if s_.npred == 0:
                    avail.add(s_.idx)
        self.makespan = max(o.finish for o in ops)
        known = {e: {} for e in self.ENGS}
        cnt = {}
        for o in order:
            kn = known[o.eng]
            waits = {}
            for d, kind in sorted(o.deps, key=lambda x: x[0].start):
                if d.eng == o.eng and (not d.is_dma) and (not o.is_dma) and o.eng == "pe":
                    continue
                k, v = d.sig
                if kn.get(k, 0) >= v:
                    continue
                waits[k] = max(waits.get(k, 0), v)
                for kk, vv in d.clock.items():
                    if kn.get(kk, 0) < vv:
                        kn[kk] = vv
            o.waits = sorted(waits.items())
            inc = 16 if o.is_dma else 1
            cnt[o.key] = cnt.get(o.key, 0) + inc
            o.sig = (o.key, cnt[o.key])
            o.clock = dict(kn)
            o.clock[o.key] = cnt[o.key]
            if o.final:
                self.final_waits[o.key] = max(self.final_waits.get(o.key, 0), cnt[o.key])
        self.cnt = cnt

    def sem_keys(self):
        return list(self.cnt.keys())

    def emit(self, eng, e, sems):
        for ins in self.q[eng]:
            waits = list(ins.waits)
            fw = None
            if ins.fuse and waits:
                fw = waits.pop()
            for k, v in waits:
                e.wait_ge(sems[k], v)
            if fw is not None and ins.fuse == 2:
                bi = ins.emit(e, (sems[fw[0]], fw[1]))
            else:
                bi = ins.emit(e)
                if fw is not None:
                    bi._wait_ge(sems[fw[0]], fw[1])
            k, v = ins.sig
            bi.then_inc(sems[k], 16 if ins.is_dma else 1)
        if eng == "sp":
            for k, v in self.final_waits.items():
                e.wait_ge(sems[k], v)


def _fd(ap):
    n = 1
    for d in ap.shape[1:]:
        n *= int(d)
    return n


ASET = {}
MODEL_DEEP = [0]
MODEL_PSUM = [0]
OPJ_OWN = [True]
TP_FIXED = [False]


def build_program(n_tiles=NTILE, do_sample=True, do_phase_a=True, n_pre=3 * NTILE, window=700, **_ignored):
    nc = bass.Bass("TRN2", target_bir_lowering=False)
    S = Sched()
    es = ExitStack()
    ASET.update({AF.Silu: "A", AF.Tanh: "A", AF.Ln: "B", AF.Exp: "B", AF.Gelu: "C", AF.Sigmoid: "D"})

    def dram_in(name, shape):
        return nc.dram_tensor(name, list(shape), F32, kind="ExternalInput").ap()

    def dram_out(name, shape):
        return nc.dram_tensor(name, list(shape), F32, kind="ExternalOutput").ap()

    xp_d = dram_in("xp", [SEG, D])
    xpre_d = dram_in("xpre", [3 * SEG, D])
    xs_d = dram_in("xs", [32, D])
    s0_d = dram_in("s0", [2, 4, 128, 128])
    vecs_d = dram_in("vecs", [28, 128])
    fnw_d = dram_in("fnw", [1, D])
    lnw_d = dram_in("lnwrow", [1, 512])
    lnb_d = dram_in("lnbrow", [1, 512])
    bs_d = dram_in("bsrow", [1, 512])
    ws_d = dram_in("ws", [4, 128, 128])
    win_d = dram_in("win", [D, INW])
    wout_d = dram_in("wout", [D, D])

    yp_d = dram_out("yp", [SEG, D])
    ys_d = dram_out("ys", [32, D])
    sp_d = dram_out("spo", [4, 128, 128])
    ss_d = dram_out("sso", [2, 4, 128, 128])
    vs_d = dram_out("vso", [32, 512])

    with es:
        def sb(name, shape, dt=F32):
            t = es.enter_context(nc.sbuf_tensor(name, list(shape), dt))
            return t, Buf(name)

        def sb4(name, shape, dt=F32):
            t = es.enter_context(nc.sbuf_tensor(name, list(shape), dt))
            return t, [Buf(f"{name}_h{h}") for h in range(4)]

        def ring(name, shape, dt=F32, depth=2):
            return [sb(f"{name}{j}", shape, dt) for j in range(depth)]

        def ring4(name, shape, dt=F32, depth=2):
            return [sb4(f"{name}{j}", shape, dt) for j in range(depth)]

        def ps(name, shape, dt=F32):
            t = es.enter_context(nc.psum_tensor(name, list(shape), dt))
            return t, Buf(name)

        Wi, _ = sb("Wi", [128, KT, INW], BF16)
        Wo, _ = sb("Wo", [128, KT, D], BF16)
        Wi_blk = [[] for j in range(7)]
        Wo_blk = [[] for j in range(2)]
        wst = [sb(f"wst{j}", [128, KT, 128], F32) for j in range(2)]

        vec_sb, vec_b = sb("vec_sb", [28, 128])
        vT, vT_b = sb("vT", [128, 28])
        lbc, lbc_b = sb("lbc", [128, 32])
        ident_f, ident_f_b = sb("ident_f", [128, 128])
        ident_bf, ident_bf_b = sb("ident_bf", [128, 128], BF16)
        mask01, mask01_b = sb("mask01", [128, 128])
        mask_s, mask_s_b = sb("mask_s", [48, 48])
        ones, ones_b = sb("ones", [128, 128])
        mhalf, mhalf_b = sb("mhalf", [128, 1])
        rmask, rmask_b = sb("rmask", [128, 512])
        rmask_s, rmask_s_b = sb("rmask_s", [128, 192])
        fnw_bc, fnw_bc_b = sb("fnw_bc", [128, D])
        lnw_bc, lnw_bc_b = sb("lnw_bc", [128, 512])
        lnb_bc, lnb_bc_b = sb("lnb_bc", [128, 512])
        bs_row, bs_row_b = sb("bs_row", [1, 512])
        bs2_row, bs2_row_b = sb("bs2_row", [1, 4, 48])
        wtmp, wtmp_b = sb("wtmp", [128, 128])
        wtmp2, wtmp2_b = sb("wtmp2", [48, 48])
        WsTf, WsTf_b = sb("WsTf", [128, 128])
        WsTsf, WsTsf_b = sb("WsTsf", [48, 48])
        WsT_bf, WsT_bf_b = sb("WsT_bf", [128, 4, 128], BF16)
        WsTs_bf, WsTs_bf_b = sb("WsTs_bf", [48, 4, 48], BF16)
        Bh, Bh_b = sb("Bh", [128, 4, 128])
        Bhs, Bhs_b = sb("Bhs", [128, 4, 48])

        R_xin = ring("xin", [128, D])
        junk_bf, junk_bf_b = sb("junk_bf", [128, D], BF16)
        junk_f, junk_f_b = sb("junk_f", [128, 128])
        R_xn = ring("xn_bf", [128, D], BF16)
        R_xnT = ring("xnT", [128, KT, 128], BF16)
        R_st1 = ring("st1", [128, 8], depth=4)
        R_st2 = ring("st2", [128, 8])
        R_tf = ring("t_f", [128, 512])
        R_logf = ring("logf", [128, 512])
        R_bcum = ring("bcum", [128, 512])
        R_enb = ring("enb", [128, 512])
        R_ebq = ring("ebq", [128, 512])
        R_sq = ring("sq", [128, 512])
        R_pk = ring("pk", [128, 2, 16], depth=4)
        R_epk = ring("epk", [128, 2, 16], depth=4)
        R_qT = ring("qT_bf", [128, 512], BF16)
        R_kkT = ring("kkT_bf", [128, 512], BF16)
        R_kk = ring("kk_bf", [128, 512], BF16)
        R_i = ring("i_bf", [128, 512], BF16)
        R_scT = ring("scT_bf", [128, 512], BF16)
        R_sga = ring("sga", [128, 512])
        R_gu = ring("gu", [128, 512])
        R_sgb = ring("sgb", [128, 512])
        R_gv = ring("gv", [128, 512])
        R_bnst = ring("bnst", [128, 8])
        vn, vn_b = sb("vn", [48, 512])
        R_vhat = ring("vhat_bf", [128, 512], BF16)
        R_ost = ring("ost", [128, 16])
        R_ya = ring4("ya_bf", [128, 512], BF16)
        R_yT = ring("yT_bf", [128, KT, 128], BF16)
        R_mix = ring4("mixtmp", [128, 512])
        R_hres = ring("hres", [128, D])

        Sp, Sp_b = sb4("Sp", [128, 512])
        Ss = [sb4(f"Ss{j}", [128, 512]) for j in range(2)]
        Smid_bf = [sb(f"Smid{j}", [128, 512], BF16) for j in range(2)]
        R_Sdec = ring("Sdec", [128, 512])
        sout, sout_b = sb("sout", [128, 512])

        ip = [ps(f"ip{j}", [128, 512]) for j in range(2)]
        tpr = [ps(f"tp{j}", [128, 1024], BF16) for j in range(2 if OPJ_OWN[0] else 3)]
        opj_own = ps("opj", [128, 512]) if OPJ_OWN[0] else None
        scp, scp_b = ps("scp", [128, 512])
        obp, obp_b = ps("obp", [128, 512])
        sup_g, sup_g_b = ps("sup", [128, 512])
        ip_rr = [0]
        tp_rr = [0]

        ip_pool = [ip]

        def next_ip():
            lst = ip_pool[0]
            j = ip_rr[0] % len(lst)
            ip_rr[0] += 1
            return lst[j]

        def next_tp(role=0):
            if TP_FIXED[0]:
                return tpr[0] if role == 0 else tpr[1]
            j = tp_rr[0] % len(tpr)
            tp_rr[0] += 1
            return tpr[j]

        def dma(out_ap, in_ap, reads, writes, key, final=False, nbytes=65536):
            def em(e):
                return e.dma_start(out=out_ap, in_=in_ap)
            S.op("sp", em, reads=reads, writes=writes, dma_key=key, final=final, dur=0.15,
                 lat=2.0 + nbytes / 150e3)

        def act(out_ap, in_ap, func, reads, writes, bias=None, scale=None, accum_out=None):
            nap = (0 if bias is None or isinstance(bias, float) else 1) + (0 if scale is None or isinstance(scale, float) else 1)

            def em(e):
                kw = {}
                if bias is not None:
                    kw["bias"] = bias
                if scale is not None:
                    kw["scale"] = scale
                if accum_out is not None:
                    kw["accum_out"] = accum_out
                return e.activation(out=out_ap, in_=in_ap, func=func, **kw)
            S.op("act", em, reads=reads, writes=writes, dur=0.2 + _fd(in_ap) / 1400.0 + 0.09 * nap + (0.1 if accum_out is not None else 0.0),
                 aset=ASET.get(func), fuse=(0 if accum_out is not None else 1))

        def veng(eng, fn, reads, writes, fd=128, kind="tt"):
            if eng == "pool":
                dur = 0.75 + fd / 480.0 if fd <= 16 else 0.45 + fd / 420.0
            elif kind == "stt" or kind == "scan":
                dur = 0.15 + fd / 480.0
            elif kind == "ts":
                dur = 0.15 + fd / 1900.0
            else:
                dur = 0.15 + fd / 960.0
            S.op(eng, fn, reads=reads, writes=writes, dur=dur, fuse=VENG_FUSE[0])

        def tt(eng, out_ap, a, b, op, reads, writes):
            veng(eng, lambda e: e.tensor_tensor(out=out_ap, in0=a, in1=b, op=op), reads, writes, _fd(out_ap))

        def ts(eng, out_ap, a, s1, s2, op0, op1, reads, writes):
            if s2 is None:
                if eng == "pool":
                    veng(eng, lambda e: e.tensor_scalar(out=out_ap, in0=a, scalar1=s1, scalar2=1.0, op0=op0, op1=ALU.mult),
                         reads, writes, _fd(out_ap))
                else:
                    veng(eng, lambda e: e.tensor_scalar(out=out_ap, in0=a, scalar1=s1, scalar2=None, op0=op0),
                         reads, writes, _fd(out_ap), "ts")
            else:
                veng(eng, lambda e: e.tensor_scalar(out=out_ap, in0=a, scalar1=s1, scalar2=s2, op0=op0, op1=op1),
                     reads, writes, _fd(out_ap))

        def stt(out_ap, a, scalar, b, op0, op1, reads, writes):
            veng("dve", lambda e: e.scalar_tensor_tensor(out=out_ap, in0=a, scalar=scalar, in1=b, op0=op0, op1=op1),
                 reads, writes, _fd(out_ap), "stt")

        def cp(eng, out_ap, in_ap, reads, writes):
            veng(eng, lambda e: e.tensor_copy(out=out_ap, in_=in_ap), reads, writes, _fd(out_ap))

        def mm_group(items, reads, writes):
            dur = 0.0
            for (o, l, r, st, sp_) in items:
                nfree = _fd(r)
                passes = 4 if r.dtype == F32 else 1
                dur += passes * (max(nfree, 64) * 0.00054 + 0.009)

            def em(e, w=None):
                bi = None
                for (o, l, r, st, sp_) in items:
                    bi = e.matmul(o, l, r, start=st, stop=sp_)
                    if w is not None:
                        bi._wait_ge(w[0], w[1])
                        w = None
                return bi
            S.op("pe", em, reads=reads, writes=writes, dur=dur, lat=dur + 0.1, fuse=2)

        def tr_group(items, reads, writes):
            dur = sum((max(_fd(i_), 64) * 0.00054 + 0.009) * (4 if i_.dtype == F32 else 1) for (o, i_, idn) in items)

            def em(e, w=None):
                bi = None
                for (o, i_, idn) in items:
                    bi = e.transpose(o, i_, idn)
                    if w is not None:
                        bi._wait_ge(w[0], w[1])
                        w = None
                return bi
            S.op("pe", em, reads=reads, writes=writes, dur=dur, lat=dur + 0.1, fuse=2)

        veng("pool", lambda e: e.memset(ident_f[:], 0.0), [], [ident_f_b])
        veng("pool", lambda e: e.affine_select(out=ident_f[:], in_=ident_f[:], pattern=[[-1, 128]],
                                               compare_op=ALU.not_equal, fill=1.0, base=0, channel_multiplier=1),
             [ident_f_b], [ident_f_b])
        cp("dve", ident_bf[:], ident_f[:], [ident_f_b], [ident_bf_b])
        veng("pool", lambda e: e.memset(mask01[:], 1.0), [], [mask01_b])
        veng("pool", lambda e: e.affine_select(out=mask01[:], in_=mask01[:], pattern=[[1, 128]],
                                               compare_op=ALU.is_ge, fill=0.0, base=0, channel_multiplier=-1),
             [mask01_b], [mask01_b])
        veng("pool", lambda e: e.memset(mask_s[:], 0.0), [], [mask_s_b])
        cp("pool", mask_s[0:16, 0:16], mask01[0:16, 0:16], [mask01_b], [mask_s_b])
        cp("pool", mask_s[32:48, 32:48], mask01[32:48, 32:48], [mask01_b], [mask_s_b])
        veng("pool", lambda e: e.memset(ones[:], 1.0), [], [ones_b])
        veng("pool", lambda e: e.memset(rmask[:], 1.0), [], [rmask_b])
        veng("pool", lambda e: e.memset(rmask[:, :].rearrange("p (h t) -> p h t", h=4)[:, :, 0:1], 0.0), [rmask_b], [rmask_b])
        veng("pool", lambda e: e.memset(rmask_s[:], 1.0), [], [rmask_s_b])
        veng("pool", lambda e: e.memset(rmask_s[:, :].rearrange("p (h t) -> p h t", h=4)[:, :, 0:1], 0.0), [rmask_s_b], [rmask_s_b])
        veng("pool", lambda e: e.memset(rmask_s[:, :].rearrange("p (h t) -> p h t", h=4)[:, :, 32:33], 0.0), [rmask_s_b], [rmask_s_b])
        veng("pool", lambda e: e.memset(mhalf[:], -0.5), [], [mhalf_b])
        veng("pool", lambda e: e.memset(wtmp2[:], 0.0), [], [wtmp2_b])
        veng("pool", lambda e: e.memset(bs2_row[:], 0.0), [], [bs2_row_b])
        veng("pool", lambda e: e.memset(Sp[:], 0.0), [], Sp_b)

        dma(vec_sb[:], vecs_d[:, :], [], [vec_b], "c0")
        dma(bs_row[:], bs_d[:, :], [], [bs_row_b], "c2")
        dma(fnw_bc[:], fnw_d.partition_broadcast(128), [], [fnw_bc_b], "c3")
        dma(lnw_bc[:], lnw_d.partition_broadcast(128), [], [lnw_bc_b], "c4")
        dma(lnb_bc[:], lnb_d.partition_broadcast(128), [], [lnb_bc_b], "c5")
        for h in range(4):
            dma(bs2_row[0:1, h, 0:16], bs_d[0:1, h * 128:h * 128 + 16], [], [bs2_row_b], "c6")
            dma(bs2_row[0:1, h, 32:48], bs_d[0:1, h * 128:h * 128 + 16], [], [bs2_row_b], "c6")

        ipt, ipt_b = ip[0]
        tr_group([(ipt[:, 0:28], vec_sb[0:28, :], ident_f[0:28, 0:28])], [vec_b, ident_f_b], [ipt_b])
        cp("dve", vT[:], ipt[:, 0:28], [ipt_b], [vT_b])
        tt("dve", lbc[:, 20:24], vT[:, 4:8], vT[:, 0:4], ALU.subtract, [vT_b], [lbc_b])
        act(lbc[:, 24:28], lbc[:, 20:24], AF.Sigmoid, [lbc_b], [lbc_b])
        ts("dve", lbc[:, 0:4], lbc[:, 24:28], 0.5, None, ALU.mult, None, [lbc_b], [lbc_b])
        ts("dve", lbc[:, 4:8], lbc[:, 0:4], -1.0, None, ALU.mult, None, [lbc_b], [lbc_b])
        ts("dve", lbc[:, 8:12], lbc[:, 0:4], -1.0, 1.0, ALU.mult, ALU.add, [lbc_b], [lbc_b])
        act(lbc[:, 12:16], lbc[:, 0:4], AF.Ln, [lbc_b], [lbc_b], scale=float(DK ** -0.5))
        veng("dve", lambda e: e.reciprocal(out=lbc[:, 16:20], in_=lbc[:, 0:4]), [lbc_b], [lbc_b])
        nBc = lambda h: lbc[:, 4 + h:5 + h]
        Acol = lambda h: lbc[:, 8 + h:9 + h]
        gnw = lambda kt: vT[:, 8 + kt:9 + kt]
        lnwc = lambda h: vT[:, 12 + h:13 + h]
        nwc = lambda kt: vT[:, 20 + kt:21 + kt]

        for h in range(4):
            dma(wtmp[:], ws_d[h, :, :], [], [wtmp_b], "c7")
            ipa, ipa_b = next_ip()
            tr_group([(ipa[:, 0:128], wtmp[:, :], ident_f[:, :])], [wtmp_b, ident_f_b], [ipa_b])
            tt("dve", WsTf[:], ipa[:, 0:128], mask01[:], ALU.mult, [ipa_b, mask01_b], [WsTf_b])
            cp("dve", WsT_bf[:, h, :], WsTf[:], [WsTf_b], [WsT_bf_b])
            ipb, ipb_b = next_ip()
            mm_group([(ipb[:, 0:128], lnb_bc[:, h * 128:(h + 1) * 128], WsTf[:], True, False),
                      (ipb[:, 0:128], ones[0:1, 0:128], bs_row[0:1, h * 128:(h + 1) * 128], False, True)],
                     [lnb_bc_b, WsTf_b, ones_b, bs_row_b], [ipb_b])
            cp("dve", Bh[:, h, :], ipb[:, 0:128], [ipb_b], [Bh_b])
            dma(wtmp2[0:16, 0:16], ws_d[h, 0:16, 0:16], [], [wtmp2_b], "c8")
            dma(wtmp2[32:48, 32:48], ws_d[h, 0:16, 0:16], [], [wtmp2_b], "c8")
            ipa, ipa_b = next_ip()
            tr_group([(ipa[0:48, 0:48], wtmp2[:, :], ident_f[0:48, 0:48])], [wtmp2_b, ident_f_b], [ipa_b])
            tt("dve", WsTsf[:], ipa[0:48, 0:48], mask_s[:], ALU.mult, [ipa_b, mask_s_b], [WsTsf_b])
            cp("dve", WsTs_bf[:, h, :], WsTsf[:], [WsTsf_b], [WsTs_bf_b])
            ipb, ipb_b = next_ip()
            mm_group([(ipb[:, 0:48], lnb_bc[0:48, h * 128:(h + 1) * 128], WsTsf[:], True, False),
                      (ipb[:, 0:48], ones[0:1, 0:128], bs2_row[0:1, h, :], False, True)],
                     [lnb_bc_b, WsTsf_b, ones_b, bs2_row_b], [ipb_b])
            cp("dve", Bhs[:, h, :], ipb[:, 0:48], [ipb_b], [Bhs_b])

        wctr = [0]
        wchunks = []
        WCH = 128

        def load_w_block(src_d, dst, dst_b, c0, width, scale_fn, extra_reads, defer=False):
            for cc in range(c0, c0 + width, WCH):
                if defer:
                    wchunks.append((src_d, dst, dst_b, cc, scale_fn, extra_reads))
                else:
                    load_w_chunk(src_d, dst, dst_b, cc, scale_fn, extra_reads)

        def load_w_chunk(src_d, dst, dst_b, cc, scale_fn, extra_reads):
            j = wctr[0] % 2
            wctr[0] += 1
            wt, wt_b = wst[j]
            dma(wt[:], src_d[:, cc:cc + WCH].rearrange("(kt p) c -> p kt c", p=128), [], [wt_b], f"w{j}",
                nbytes=128 * KT * WCH * 4)
            for kt in range(KT):
                eng = "pool" if (kt % 4 == 3) else "dve"
                q_ = 1 if eng == "pool" else 0
                sc = scale_fn(kt)
                wb = Buf(f"w_{cc}_{kt}")
                dst_b.append(wb)
                if sc is None:
                    cp(eng, dst[:, kt, cc:cc + WCH], wt[:, kt, :], [wt_b], [wb])
                else:
                    ts(eng, dst[:, kt, cc:cc + WCH], wt[:, kt, :], sc, None, ALU.mult, None,
                       [wt_b] + extra_reads, [wb])

        for blk in (1, 2):
            load_w_block(win_d, Wi, Wi_blk[blk], blk * 512, 512, nwc, [vT_b])

        tctr = [0]

        def tile_pass(x_src, TT, segs, states, full, WsT_use, Bh_use, out_dst, v_dst, rings):
            W4 = 4 * TT
            sup0, sup0_b = sup_g, sup_g_b
            sup, sup_b = sup_g, sup_g_b
            ti = tctr[0]
            tctr[0] += 1
            pick = lambda r: r[ti % len(r)]
            RR = rings
            xt, xt_b = pick(RR["xin"])
            xT, xT_b = pick(RR["xnT"])
            xn_bf, xn_bf_b = pick(R_xn)
            st1, st1_b = pick(R_st1)
            st2, st2_b = pick(R_st2)
            t_f, t_f_b = pick(RR["tf"])
            logf, logf_b = pick(RR["logf"])
            bcum, bcum_b = pick(RR["bcum"])
            enb, enb_b = pick(RR["enb"])
            ebq, ebq_b = pick(R_ebq)
            sq, sq_b = pick(R_sq)
            pk, pk_b = pick(R_pk)
            epk, epk_b = pick(R_epk)
            qT_bf, qT_bf_b = pick(R_qT)
            kkT_bf, kkT_bf_b = pick(RR["kkT"])
            kk_bf, kk_bf_b = pick(RR["kk"])
            i_bf, i_bf_b = pick(RR["i"])
            scT_bf, scT_bf_b = pick(R_scT)
            sga, sga_b = pick(R_sga)
            gu, gu_b = pick(R_gu)
            sgb, sgb_b = pick(R_sgb)
            gv, gv_b = pick(R_gv)
            bnst, bnst_b = pick(R_bnst)
            vhat_bf, vhat_bf_b = pick(R_vhat)
            ost, ost_b = pick(R_ost)
            ya_bf, ya_bf_b = pick(R_ya)
            yT_bf, yT_bf_b = pick(R_yT)
            mixtmp, mixtmp_b = pick(R_mix)
            hres, hres_b = pick(R_hres)
            Sdec, Sdec_b = pick(RR["Sdec"])
            xkey = "x_" + xt_b.name

            if TT == 128:
                dma(xt[:], x_src, [], [xt_b], xkey, nbytes=128 * D * 4)
            else:
                veng("pool", lambda e: e.memset(xt[0:TT, :], 0.0), [], [xt_b], D)
                dma(xt[0:16, :], x_src[0:16, :], [], [xt_b], xkey)
                dma(xt[32:48, :], x_src[16:32, :], [], [xt_b], xkey)
            act(xn_bf[0:TT, :], xt[0:TT, :], AF.Square, [xt_b], [st1_b, xn_bf_b], accum_out=st1[0:TT, 0:1])
            ts("dve", st1[0:TT, 1:2], st1[0:TT, 0:1], 1.0 / D, EPS, ALU.mult, ALU.add, [st1_b], [st1_b])
            tt("pool", st1[0:TT, 2:3], st1[0:TT, 1:2], mhalf[0:TT, :], ALU.pow, [st1_b, mhalf_b], [st1_b])
            ts("dve", xn_bf[0:TT, :], xt[0:TT, :], st1[0:TT, 2:3], None, ALU.mult, None, [xt_b, st1_b], [xn_bf_b])
            if full:
                tp, tp_b = next_tp(0)
            elif ti % 2 == 0:
                tp, tp_b = tpr[0]
            else:
                tp, tp_b = obp[:, :].bitcast(BF16), obp_b
            tr_group([(tp[:, kt * 128:kt * 128 + TT], xn_bf[0:TT, kt * 128:(kt + 1) * 128], ident_bf[0:TT, 0:TT])
                      for kt in range(KT)], [xn_bf_b, ident_bf_b], [tp_b])
            if full:
                cp("dve", xT[:, :, 0:TT], tp[:, :].rearrange("p (k t) -> p k t", k=KT)[:, :, 0:TT], [tp_b], [xT_b])
            else:
                act(xT[:, :, 0:TT], tp[:, :].rearrange("p (k t) -> p k t", k=KT)[:, :, 0:TT], AF.Copy, [tp_b], [xT_b])

            def proj_fm(c0, blk_b):
                bank, bank_b = next_ip()
                items = []
                for h in range(4):
                    for kt in range(KT):
                        items.append((bank[:, h * TT:(h + 1) * TT], Wi[:, kt, c0 + h * 128:c0 + (h + 1) * 128],
                                      xT[:, kt, 0:TT], kt == 0, kt == KT - 1))
                mm_group(items, [xT_b] + blk_b, [bank_b])
                return bank, bank_b

            def proj_tm(c0, blk_b):
                bank, bank_b = next_ip()
                items = [(bank[0:TT, :], xT[:, kt, 0:TT], Wi[:, kt, c0:c0 + 512], kt == 0, kt == KT - 1)
                         for kt in range(KT)]
                mm_group(items, [xT_b] + blk_b, [bank_b])
                return bank, bank_b

            if full:
                qb, qb_b = proj_fm(C_Q, Wi_blk[0])
                act(sq[:, 0:W4], qb[:, 0:W4], AF.Silu, [qb_b], [sq_b])
            fb, fb_b = proj_fm(C_F, Wi_blk[1])
            act(t_f[:, 0:W4], fb[:, 0:W4], AF.Tanh, [fb_b], [t_f_b], scale=-0.5)
            if full:
                gab, gab_b = proj_tm(C_GA, Wi_blk[3])
                act(sga[0:TT, :], gab[0:TT, :], AF.Silu, [gab_b], [sga_b])
                gbb, gbb_b = proj_fm(C_GB, Wi_blk[6])
                act(sgb[:, 0:W4], gbb[:, 0:W4], AF.Silu, [gbb_b], [sgb_b])
            ib_, ib_b = proj_tm(C_I, Wi_blk[2])
            if full:
                cp("dve", i_bf[0:TT, :], ib_[0:TT, :], [ib_b], [i_bf_b])
            else:
                act(i_bf[0:TT, :], ib_[0:TT, :], AF.Copy, [ib_b], [i_bf_b])

            v4 = lambda t_: t_[:, 0:W4].rearrange("p (h t) -> p h t", h=4)
            tt("pool", v4(logf), v4(t_f), lbc[:, 4:8].unsqueeze(2).to_broadcast([128, 4, TT]), ALU.mult,
               [t_f_b, lbc_b], [logf_b])
            tt("pool", v4(logf), v4(logf), lbc[:, 8:12].unsqueeze(2).to_broadcast([128, 4, TT]), ALU.add,
               [logf_b, lbc_b], [logf_b])
            act(logf[:, 0:W4], logf[:, 0:W4], AF.Ln, [logf_b], [logf_b])
            bc3 = v4(bcum)
            rm = rmask if TT == 128 else rmask_s
            rm_b = rmask_b if TT == 128 else rmask_s_b
            veng("dve", lambda e: e.tensor_tensor_scan(
                out=bcum[:, 0:W4], data0=rm[:, 0:W4], data1=logf[:, 0:W4], initial=0.0, op0=ALU.mult, op1=ALU.add),
                [logf_b, rm_b], [bcum_b], W4, "scan")
            for si, (r0, n) in enumerate(segs):
                mid = r0 + n // 2 - 1
                last = r0 + n - 1
                if not full:
                    cp("dve", pk[:, si, 4:8], bc3[:, :, last], [bcum_b], [pk_b])
                    act(epk[:, si, 4:8], pk[:, si, 4:8], AF.Exp, [pk_b], [epk_b])
                    tt("pool", v4(enb)[:, :, r0:r0 + n], bc3[:, :, r0:r0 + n],
                       pk[:, si, 4:8].unsqueeze(2).to_broadcast([128, 4, n]), ALU.subtract, [bcum_b, pk_b], [enb_b])
                    continue
                cp("dve", pk[:, si, 0:4], bc3[:, :, mid], [bcum_b], [pk_b])
                cp("dve", pk[:, si, 4:8], bc3[:, :, last], [bcum_b], [pk_b])
                tt("dve", pk[:, si, 8:12], pk[:, si, 4:8], pk[:, si, 0:4], ALU.subtract, [pk_b], [pk_b])
                tt("dve", pk[:, si, 12:16], lbc[:, 12:16], pk[:, si, 0:4], ALU.subtract, [pk_b, lbc_b], [pk_b])
                act(epk[:, si, 0:12], pk[:, si, 0:12], AF.Exp, [pk_b], [epk_b])
                tt("pool", v4(enb)[:, :, r0:r0 + n], bc3[:, :, r0:r0 + n],
                   pk[:, si, 0:4].unsqueeze(2).to_broadcast([128, 4, n]), ALU.subtract, [bcum_b, pk_b], [enb_b])
                tt("pool", v4(ebq)[:, :, r0:r0 + n], bc3[:, :, r0:r0 + n],
                   pk[:, si, 12:16].unsqueeze(2).to_broadcast([128, 4, n]), ALU.add, [bcum_b, pk_b], [ebq_b])
            if len(segs) > 1:
                veng("pool", lambda e: e.memset(v4(enb)[:, :, 16:32], 0.0), [enb_b], [enb_b], 64)
                if full:
                    veng("pool", lambda e: e.memset(v4(ebq)[:, :, 16:32], 0.0), [ebq_b], [ebq_b], 64)
            act(enb[:, 0:W4], enb[:, 0:W4], AF.Exp, [enb_b], [enb_b], scale=-1.0)
            if full:
                act(ebq[:, 0:W4], ebq[:, 0:W4], AF.Exp, [ebq_b], [ebq_b])
            stt(kkT_bf[:, 0:W4], t_f[:, 0:W4], 1.0, enb[:, 0:W4], ALU.add, ALU.mult, [t_f_b, enb_b], [kkT_bf_b])
            if full:
                tt("dve", qT_bf[:, 0:W4], sq[:, 0:W4], ebq[:, 0:W4], ALU.mult, [sq_b, ebq_b], [qT_bf_b])
            if full:
                tp, tp_b = next_tp(1)
            else:
                tp, tp_b = tpr[1]
            tr_group([(tp[0:TT, h * 128:(h + 1) * 128], kkT_bf[:, h * TT:(h + 1) * TT], ident_bf[:, :])
                      for h in range(4)], [kkT_bf_b, ident_bf_b], [tp_b])
            if full:
                cp("dve", kk_bf[0:TT, :], tp[0:TT, 0:512], [tp_b], [kk_bf_b])
            else:
                act(kk_bf[0:TT, :], tp[0:TT, 0:512], AF.Copy, [tp_b], [kk_bf_b])

            for si, (r0, n) in enumerate(segs):
                St, St_b, Sm, Sm_b = states[si]
                if full:
                    veng("pool", lambda e, St=St, Sm=Sm, si=si: e.tensor_tensor(
                        out=Sm[:, :].rearrange("p (h v) -> p h v", h=4),
                        in0=St[:, :].rearrange("p (h v) -> p h v", h=4),
                        in1=epk[:, si, 0:4].unsqueeze(2).to_broadcast([128, 4, 128]), op=ALU.mult),
                        St_b + [epk_b], [Sm_b], 512)
            if full:
                items = []
                for h in range(4):
                    items.append((scp[0:TT, h * TT:(h + 1) * TT], kkT_bf[:, h * TT:(h + 1) * TT],
                                  qT_bf[:, h * TT:(h + 1) * TT], True, True))
                mm_group(items, [kkT_bf_b, qT_bf_b], [scp_b])
                msk = mask01 if TT == 128 else mask_s
                msk_b = mask01_b if TT == 128 else mask_s_b
                veng("dve", lambda e: e.tensor_tensor(
                    out=scT_bf[0:TT, 0:W4].rearrange("p (h t) -> p h t", h=4),
                    in0=scp[0:TT, 0:W4].rearrange("p (h t) -> p h t", h=4),
                    in1=msk[0:TT, 0:TT].unsqueeze(1).to_broadcast([TT, 4, TT]), op=ALU.mult),
                    [scp_b, msk_b], [scT_bf_b], W4)
                items = []
                smr = []
                for h in range(4):
                    for si, (r0, n) in enumerate(segs):
                        St, St_b, Sm, Sm_b = states[si]
                        items.append((obp[r0:r0 + n, h * 128:(h + 1) * 128],
                                      scT_bf[0:TT, h * TT + r0:h * TT + r0 + n],
                                      i_bf[0:TT, h * 128:(h + 1) * 128], True, False))
                        items.append((obp[r0:r0 + n, h * 128:(h + 1) * 128], qT_bf[:, h * TT + r0:h * TT + r0 + n],
                                      Sm[:, h * 128:(h + 1) * 128], False, True))
                        smr.append(Sm_b)
                mm_group(items, [scT_bf_b, i_bf_b, qT_bf_b] + smr, [obp_b])
            for si, (r0, n) in enumerate(segs):
                St, St_b, Sm, Sm_b = states[si]
                if False and (not full) and (ti % 2 == 1):
                    sup, sup_b = opj_own
                else:
                    sup, sup_b = sup0, sup0_b
                items = [(sup[:, h * 128:(h + 1) * 128], kk_bf[r0:r0 + n, h * 128:(h + 1) * 128],
                          i_bf[r0:r0 + n, h * 128:(h + 1) * 128], True, True) for h in range(4)]
                mm_group(items, [kk_bf_b, i_bf_b], [sup_b])
                if full:
                    veng("pool", lambda e, St=St, si=si: e.tensor_tensor(
                        out=Sdec[:, :].rearrange("p (h v) -> p h v", h=4),
                        in0=St[:, :].rearrange("p (h v) -> p h v", h=4),
                        in1=epk[:, si, 4:8].unsqueeze(2).to_broadcast([128, 4, 128]), op=ALU.mult),
                        St_b + [epk_b], [Sdec_b], 512)
                if full:
                    for h in range(4):
                        stt(St[:, h * 128:(h + 1) * 128], sup[:, h * 128:(h + 1) * 128], epk[:, si, 8 + h:9 + h],
                            Sdec[:, h * 128:(h + 1) * 128], ALU.mult, ALU.add, [sup_b, epk_b, Sdec_b], [St_b[h]])
                else:
                    for h in range(4):
                        stt(St[:, h * 128:(h + 1) * 128], St[:, h * 128:(h + 1) * 128], epk[:, si, 4 + h:5 + h],
                            sup[:, h * 128:(h + 1) * 128], ALU.mult, ALU.add, [sup_b, epk_b, St_b[h]], [St_b[h]])
            if not full:
                return

            for h in range(4):
                act(ya_bf[0:TT, h * 128:(h + 1) * 128], obp[0:TT, h * 128:(h + 1) * 128], AF.Square, [obp_b],
                    [ost_b, ya_bf_b[h]], accum_out=ost[0:TT, h:h + 1])
            ts("dve", ost[0:TT, 4:8], ost[0:TT, 0:4], 1.0 / 128, EPS, ALU.mult, ALU.add, [ost_b], [ost_b])
            tt("pool", ost[0:TT, 8:12], ost[0:TT, 4:8], mhalf[0:TT, 0:1].to_broadcast([TT, 4]), ALU.pow,
               [ost_b, mhalf_b], [ost_b])
            for h in range(4):
                stt(ya_bf[0:TT, h * 128:(h + 1) * 128], obp[0:TT, h * 128:(h + 1) * 128], ost[0:TT, 8 + h:9 + h],
                    sga[0:TT, h * 128:(h + 1) * 128], ALU.mult, ALU.mult, [obp_b, ost_b, sga_b], [ya_bf_b[h]])
            tp, tp_b = next_tp(1)
            tr_group([(tp[:, h * 128:h * 128 + TT], ya_bf[0:TT, h * 128:(h + 1) * 128], ident_bf[0:TT, 0:TT])
                      for h in range(4)], ya_bf_b + [ident_bf_b], [tp_b])
            cp("dve", yT_bf[:, 0:4, 0:TT], tp[:, 0:512].rearrange("p (k t) -> p k t", k=4)[:, :, 0:TT],
               [tp_b], [yT_bf_b])

            vb, vb_b = proj_tm(C_V, Wi_blk[5])
            act(gv[0:TT, :], vb[0:TT, :], AF.Gelu, [vb_b], [gv_b])
            ub, ub_b = proj_fm(C_U, Wi_blk[4])
            act(gu[:, 0:W4], ub[:, 0:W4], AF.Gelu, [ub_b], [gu_b])
            veng("dve", lambda e: e.bn_stats(out=bnst[0:TT, 0:6], in_=gv[0:TT, :]), [gv_b], [bnst_b], 512)
            veng("dve", lambda e: e.bn_aggr(out=bnst[0:TT, 6:8], in_=bnst[0:TT, 0:6]), [bnst_b], [bnst_b])
            ts("dve", st2[0:TT, 4:5], bnst[0:TT, 7:8], EPS, None, ALU.add, None, [bnst_b], [st2_b])
            tt("pool", st2[0:TT, 5:6], st2[0:TT, 4:5], mhalf[0:TT, :], ALU.pow, [st2_b, mhalf_b], [st2_b])
            ts("dve", vhat_bf[0:TT, :], gv[0:TT, :], bnst[0:TT, 6:7], st2[0:TT, 5:6], ALU.subtract, ALU.mult,
               [gv_b, bnst_b, st2_b], [vhat_bf_b])
            if v_dst is not None:
                ts("dve", vn[0:TT, :], gv[0:TT, :], bnst[0:TT, 6:7], st2[0:TT, 5:6], ALU.subtract, ALU.mult,
                   [gv_b, bnst_b, st2_b], [vn_b])
                tt("pool", vn[0:TT, :], vn[0:TT, :], lnw_bc[0:TT, :], ALU.mult, [vn_b, lnw_bc_b], [vn_b])
                tt("pool", vn[0:TT, :], vn[0:TT, :], lnb_bc[0:TT, :], ALU.add, [vn_b, lnb_bc_b], [vn_b])
                dma(v_dst[0:16, :], vn[0:16, :], [vn_b], [], "vs", final=True)
                dma(v_dst[16:32, :], vn[32:48, :], [vn_b], [], "vs", final=True)
            items = [(sup[:, h * TT:(h + 1) * TT], vhat_bf[0:TT, h * 128:(h + 1) * 128], WsT_use[0:TT, h, 0:TT],
                      True, True) for h in range(4)]
            mm_group(items, [vhat_bf_b, WsT_bf_b, WsTs_bf_b], [sup_b])
            for h in range(4):
                stt(mixtmp[:, h * TT:(h + 1) * TT], sup[:, h * TT:(h + 1) * TT], lnwc(h), Bh_use[:, h, 0:TT],
                    ALU.mult, ALU.add, [sup_b, vT_b, Bh_b, Bhs_b], [mixtmp_b[h]])
            tt("pool", mixtmp[:, 0:W4], mixtmp[:, 0:W4], gu[:, 0:W4], ALU.mult, mixtmp_b + [gu_b], mixtmp_b)
            veng("dve", lambda e: e.tensor_tensor(
                out=yT_bf[:, 4:8, 0:TT], in0=mixtmp[:, 0:W4].rearrange("p (h t) -> p h t", h=4),
                in1=sgb[:, 0:W4].rearrange("p (h t) -> p h t", h=4), op=ALU.mult),
                mixtmp_b + [sgb_b], [yT_bf_b], W4)

            for half in range(2):
                opj, opj_b = (opj_own if OPJ_OWN[0] else next_ip())
                items = [(opj[0:TT, :], yT_bf[:, kt, 0:TT], Wo[:, kt, half * 512:(half + 1) * 512],
                          kt == 0, kt == KT - 1) for kt in range(KT)]
                mm_group(items, [yT_bf_b] + Wo_blk[half], [opj_b])
                tt("dve", hres[0:TT, half * 512:(half + 1) * 512], opj[0:TT, :], xt[0:TT, half * 512:(half + 1) * 512],
                   ALU.add, [opj_b, xt_b], [hres_b])
            act(xn_bf[0:TT, :], hres[0:TT, :], AF.Square, [hres_b], [st2_b, xn_bf_b], accum_out=st2[0:TT, 0:1])
            ts("dve", st2[0:TT, 1:2], st2[0:TT, 0:1], 1.0 / D, EPS, ALU.mult, ALU.add, [st2_b], [st2_b])
            tt("pool", st2[0:TT, 2:3], st2[0:TT, 1:2], mhalf[0:TT, :], ALU.pow, [st2_b, mhalf_b], [st2_b])
            stt(hres[0:TT, :], hres[0:TT, :], st2[0:TT, 2:3], fnw_bc[0:TT, :], ALU.mult, ALU.mult,
                [hres_b, st2_b, fnw_bc_b], [hres_b])
            if TT == 128:
                dma(out_dst, hres[:, :], [hres_b], [], "y_" + hres_b.name, final=True, nbytes=128 * D * 4)
            else:
                dma(out_dst[0:16, :], hres[0:16, :], [hres_b], [], "y_" + hres_b.name, final=True)
                dma(out_dst[16:32, :], hres[32:48, :], [hres_b], [], "y_" + hres_b.name, final=True)

        prompt_state = [(Sp, Sp_b, Smid_bf[0][0], Smid_bf[0][1])]
        R_xin_b = R_xin + [(wst[0][0][:, :, :].rearrange("p k c -> p (k c)"), wst[0][1]),
                           (wst[1][0][:, :, :].rearrange("p k c -> p (k c)"), wst[1][1])]
        RB = {"xin": R_xin_b, "xnT": R_xnT, "tf": R_tf, "logf": R_logf, "bcum": R_bcum, "enb": R_enb,
              "kkT": R_kkT, "kk": R_kk, "i": R_i, "Sdec": R_Sdec}
        if MODEL_DEEP[0] > 0:
            nc2 = bass.Bass("TRN2", target_bir_lowering=False)
            if MODEL_PSUM[0]:
                for j in range(4):
                    tpr.append((es.enter_context(nc2.psum_tensor(f"tpx{j}", [128, 1024], BF16)), Buf(f"tpx{j}")))

            def extra(name, shape, dt, k):
                return [(es.enter_context(nc2.sbuf_tensor(f"{name}_x{j}", list(shape), dt)), Buf(f"{name}_x{j}")) for j in range(k)]
            k_ = MODEL_DEEP[0]
            R_hres = R_hres + extra("xin", [128, D], F32, k_)
            R_yT = R_yT + extra("xnT", [128, KT, 128], BF16, k_)
            R_sq = R_sq + extra("tf", [128, 512], F32, k_)
            R_ebq = R_ebq + extra("logf", [128, 512], F32, k_)
            R_sga = R_sga + extra("bcum", [128, 512], F32, k_)
            R_gu = R_gu + extra("enb", [128, 512], F32, k_)
            R_qT = R_qT + extra("kkT", [128, 512], BF16, k_)
            R_scT = R_scT + extra("kk", [128, 512], BF16, k_)
            R_vhat = R_vhat + extra("i", [128, 512], BF16, k_)
            R_xn.extend(extra("xn", [128, D], BF16, k_))
            R_st1.extend(extra("st1", [128, 8], F32, k_))
            R_pk.extend(extra("pk", [128, 2, 16], F32, k_))
            R_epk.extend(extra("epk", [128, 2, 16], F32, k_))
        RA = {"xin": R_xin + R_hres, "xnT": R_xnT + R_yT, "tf": R_tf + R_sq, "logf": R_logf + R_ebq,
              "bcum": R_bcum + R_sga, "enb": R_enb + R_gu, "kkT": R_kkT + R_qT, "kk": R_kk + R_scT,
              "i": R_i + R_vhat, "Sdec": R_Sdec + R_sgb}

        for blk in (0, 3, 6, 5, 4):
            load_w_block(win_d, Wi, Wi_blk[blk], blk * 512, 512, nwc, [vT_b], defer=True)
        for half in range(2):
            load_w_block(wout_d, Wo, Wo_blk[half], half * 512, 512, lambda kt: (gnw(kt) if kt < 4 else None), [vT_b],
                         defer=True)
        if do_phase_a:
            ip_pool[0] = ip + [(scp, scp_b), opj_own]
            for t in range(n_pre):
                tile_pass(xpre_d[t * 128:(t + 1) * 128, :], 128, [(0, 128)], prompt_state, False,
                          None, None, None, None, RA)
                if wchunks:
                    load_w_chunk(*wchunks.pop(0))
        while wchunks:
            load_w_chunk(*wchunks.pop(0))
        ip_pool[0] = ip

        for t in range(n_tiles):
            tile_pass(xp_d[t * 128:(t + 1) * 128, :], 128, [(0, 128)], prompt_state, True,
                      WsT_bf, Bh, yp_d[t * 128:(t + 1) * 128, :], None, RB)
        veng("dve", lambda e: e.tensor_tensor(
            out=sout[:, :].rearrange("p (h v) -> p h v", h=4),
            in0=Sp[:, :].rearrange("p (h v) -> p h v", h=4),
            in1=lbc[:, 0:4].unsqueeze(2).to_broadcast([128, 4, 128]), op=ALU.mult),
            Sp_b + [lbc_b], [sout_b], 512)
        dma(sp_d.rearrange("h k v -> k h v"), sout[:, :].rearrange("p (h v) -> p h v", h=4), [sout_b], [], "spo",
            final=True, nbytes=262144)

        if do_sample:
            sst = []
            for q_ in range(2):
                St, St_b = Ss[q_]
                dma(St[:, :].rearrange("p (h v) -> p h v", h=4), s0_d[q_].rearrange("h k v -> k h v"), [], St_b,
                    f"s0{q_}", nbytes=262144)
                veng("dve", lambda e, St=St: e.tensor_tensor(
                    out=St[:, :].rearrange("p (h v) -> p h v", h=4),
                    in0=St[:, :].rearrange("p (h v) -> p h v", h=4),
                    in1=lbc[:, 16:20].unsqueeze(2).to_broadcast([128, 4, 128]), op=ALU.mult),
                    St_b + [lbc_b], St_b, 512)
                sst.append((St, St_b, Smid_bf[q_][0], Smid_bf[q_][1]))
            tile_pass(xs_d, 48, [(0, 16), (32, 16)], sst, True, WsTs_bf, Bhs, ys_d, vs_d, RB)
            for q_ in range(2):
                St, St_b = Ss[q_]
                veng("dve", lambda e, St=St: e.tensor_tensor(
                    out=sout[:, :].rearrange("p (h v) -> p h v", h=4),
                    in0=St[:, :].rearrange("p (h v) -> p h v", h=4),
                    in1=lbc[:, 0:4].unsqueeze(2).to_broadcast([128, 4, 128]), op=ALU.mult),
                    St_b + [lbc_b], [sout_b], 512)
                dma(ss_d[q_].rearrange("h k v -> k h v"), sout[:, :].rearrange("p (h v) -> p h v", h=4),
                    [sout_b], [], "sso", final=True, nbytes=262144)

        build_program.sbuf_free = nc.sbuf_bytes_remaining
        S.schedule(window=window)
        build_program.last_sched = S
        if MODEL_DEEP[0] > 0:
            return None
        sems = {k: es.enter_context(nc.semaphore("s_" + k)) for k in S.sem_keys()}
        with nc.allow_non_contiguous_dma(reason="small strided constant / state layouts"):
            with nc.Block() as block:
                @block.sync
                def _(e):
                    S.emit("sp", e, sems)

                @block.tensor
                def _(e):
                    S.emit("pe", e, sems)

                @block.scalar
                def _(e):
                    S.emit("act", e, sems)

                @block.vector
                def _(e):
                    S.emit("dve", e, sems)

                @block.gpsimd
                def _(e):
                    S.emit("pool", e, sems)
    build_program.last_sched = S
    return nc


_NC_CACHE = {}


def _get_nc():
    if "nc" not in _NC_CACHE:
        _NC_CACHE["nc"] = build_program()
    return _NC_CACHE["nc"]


def make_in_maps(x_prompt, x_sample, state_hgrn, norm_w, w_in, lb_logits, g_norm_w, ln_v_w, ln_v_b,
                 w_s, b_s, w_out, final_norm_w):
    f = lambda a: np.ascontiguousarray(np.asarray(a, dtype=np.float32))
    x_prompt, x_sample, state_hgrn = f(x_prompt), f(x_sample), f(state_hgrn)
    vecs = np.concatenate([
        f(lb_logits).reshape(8, 128), f(g_norm_w).reshape(4, 128), f(ln_v_w).reshape(4, 128),
        f(ln_v_b).reshape(4, 128), f(norm_w).reshape(8, 128)], axis=0)
    shared = {
        "vecs": f(vecs),
        "fnw": f(final_norm_w).reshape(1, D),
        "lnwrow": f(ln_v_w).reshape(1, 512),
        "lnbrow": f(ln_v_b).reshape(1, 512),
        "bsrow": f(b_s).reshape(1, 512),
        "ws": f(w_s).reshape(4, 128, 128),
        "win": f(w_in).reshape(D, INW),
        "wout": f(w_out).reshape(D, D),
    }
    in_maps = []
    for c in range(NCORES):
        b, s = c // 4, c % 4
        m = dict(shared)
        m["xp"] = f(x_prompt[b, s * SEG:(s + 1) * SEG, :])
        pre = np.zeros((3 * SEG, D), np.float32)
        if s > 0:
            pre[(3 - s) * SEG:, :] = x_prompt[b, 0:s * SEG, :]
        m["xpre"] = pre
        m["xs"] = f(x_sample[2 * c:2 * c + 2].reshape(32, D))
        m["s0"] = f(state_hgrn[0, 2 * c:2 * c + 2])
        in_maps.append(m)
    return in_maps


def kernel(x_prompt, x_sample, state_hgrn, norm_w, w_in, lb_logits, g_norm_w, ln_v_w, ln_v_b,
           w_s, b_s, w_out, final_norm_w):
    in_maps = make_in_maps(x_prompt, x_sample, state_hgrn, norm_w, w_in, lb_logits, g_norm_w, ln_v_w, ln_v_b,
                           w_s, b_s, w_out, final_norm_w)
    nc = _get_nc()
    res = run_bass_kernel_spmd(nc, in_maps, core_ids=list(range(NCORES)))
    r = res.results
    y_prompt = np.zeros((2, 8192, D), np.float32)
    y_sample = np.zeros((16, 16, D), np.float32)
    sp = np.zeros((1, 2, 4, 128, 128), np.float32)
    ss = np.zeros((1, 16, 4, 128, 128), np.float32)
    vs = np.zeros((1, 16, 16, 512), np.float32)
    for c in range(NCORES):
        b, s = c // 4, c % 4
        y_prompt[b, s * SEG:(s + 1) * SEG, :] = r[c]["yp"]
        y_sample[2 * c:2 * c + 2] = r[c]["ys"].reshape(2, 16, D)
        ss[0, 2 * c:2 * c + 2] = r[c]["sso"]
        vs[0, 2 * c:2 * c + 2] = r[c]["vso"].reshape(2, 16, 512)
        if s == 3:
            sp[0, b] = r[c]["spo"]
    return (y_prompt, y_sample, sp, ss, vs)
```

```python
import numpy as np
from contextlib import ExitStack

import concourse.bass as bass
import concourse.mybir as mybir
from concourse.bass_utils import run_bass_kernel_spmd

F32 = mybir.dt.float32
BF16 = mybir.dt.bfloat16
AF = mybir.ActivationFunctionType
ALU = mybir.AluOpType

NCORES = 8
D = 1024
KT = 8
INW = 3584
SEG = 2048
NTILE = SEG // 128
EPS = 1e-6
DK = 128
C_Q, C_F, C_I, C_GA, C_U, C_V, C_GB = 0, 512, 1024, 1536, 2048, 2560, 3072


class Buf:
    def __init__(self, name):
        self.name = name
        self.last_w = None
        self.readers = []


class Ins:
    __slots__ = ("idx", "eng", "emit", "deps", "waits", "sig", "clock", "is_dma", "key", "dur", "lat", "aset",
                 "final", "start", "finish", "nsucc", "succs", "npred", "fuse")


ACT_SWITCH_US = 1.3
ACT_SWITCH_DECISION_US = [1.3]
HOP_US = 1.0
FUSE_WAITS = [True]
VENG_FUSE = [1]


class Sched:
    ENGS = ("pe", "act", "dve", "pool", "sp")

    def __init__(self):
        self.ops = []
        self.final_waits = {}

    def op(self, eng, emit, reads=(), writes=(), dma_key=None, final=False, dur=0.3, lat=None, aset=None, fuse=0):
        ins = Ins()
        ins.fuse = fuse if FUSE_WAITS[0] else 0
        ins.idx = len(self.ops)
        ins.eng = eng
        ins.emit = emit
        ins.is_dma = dma_key is not None
        ins.key = dma_key if ins.is_dma else eng
        ins.dur = dur
        ins.lat = dur if lat is None else lat
        ins.aset = aset
        ins.final = final
        deps = {}
        for b in reads:
            if b.last_w is not None and b.last_w is not ins:
                deps[b.last_w.idx] = (b.last_w, "RAW")
        for b in writes:
            if b.last_w is not None and b.last_w is not ins:
                if b.last_w.idx not in deps:
                    deps[b.last_w.idx] = (b.last_w, "WAW")
            for r in b.readers:
                if r is not ins and r.idx not in deps:
                    deps[r.idx] = (r, "WAR")
        ins.deps = list(deps.values())
        for b in writes:
            b.last_w = ins
            b.readers = []
        for b in reads:
            if b.last_w is not ins:
                b.readers.append(ins)
        self.ops.append(ins)
        return ins

    def schedule(self, window=700):
        ops = self.ops
        n = len(ops)
        for o in ops:
            o.succs = []
            o.npred = len(o.deps)
            o.start = None
        for o in ops:
            for d, _ in o.deps:
                d.succs.append(o)
        free = {e: 0.0 for e in self.ENGS}
        cur_set = [None]
        avail = set(o.idx for o in ops if o.npred == 0)
        done = 0
        lo = 0
        order = []
        self.q = {e: [] for e in self.ENGS}
        while done < n:
            while lo < n and ops[lo].start is not None:
                lo += 1
            best = None
            best_t = None
            for i in avail:
                if i >= lo + window:
                    continue
                o = ops[i]
                rdy = 0.0
                for d, kind in o.deps:
                    f = d.finish
                    if d.eng != o.eng or d.is_dma or o.is_dma:
                        f += HOP_US
                    if f > rdy:
                        rdy = f
                t = max(rdy, free[o.eng])
                tdec = t
                if o.eng == "act" and o.aset is not None and cur_set[0] is not None and o.aset != cur_set[0]:
                    t += ACT_SWITCH_US
                    tdec = t - ACT_SWITCH_US + ACT_SWITCH_DECISION_US[0]
                if best is None or tdec < best_d - 1e-9 or (abs(tdec - best_d) <= 1e-9 and i < best.idx):
                    best, best_t, best_d = o, t, tdec
            o = best
            avail.discard(o.idx)
            if o.eng == "act" and o.aset is not None:
                cur_set[0] = o.aset
            o.start = best_t
            free[o.eng] = best_t + o.dur
            o.finish = best_t + o.lat
            order.append(o)
            self.q[o.eng].append(o)
            done += 1
            for s_ in o.succs:
                s_.npred -= 1
                if s_.npred == 0:
                    avail.add(s_.idx)
        self.makespan = max(o.finish for o in ops)
        known = {e: {} for e in self.ENGS}
        cnt = {}
        for o in order:
            kn = known[o.eng]
            waits = {}
            for d, kind in sorted(o.deps, key=lambda x: x[0].start):
                if d.eng == o.eng and (not d.is_dma) and (not o.is_dma) and o.eng == "pe":
                    continue
                k, v = d.sig
                if kn.get(k, 0) >= v:
                    continue
                waits[k] = max(waits.get(k, 0), v)
                for kk, vv in d.clock.items():
                    if kn.get(kk, 0) < vv:
                        kn[kk] = vv
            o.waits = sorted(waits.items())
            inc = 16 if o.is_dma else 1
            cnt[o.key] = cnt.get(o.key, 0) + inc
            o.sig = (o.key, cnt[o.key])
            o.clock = dict(kn)
            o.clock[o.key] = cnt[o.key]
            if o.final:
                self.final_waits[o.key] = max(self.final_waits.get(o.key, 0), cnt[o.key])
        self.cnt = cnt

    def sem_keys(self):
        return list(self.cnt.keys())

    def emit(self, eng, e, sems):
        for ins in self.q[eng]:
            waits = list(ins.waits)
            fw = None
            if ins.fuse and waits:
                fw = waits.pop()
            for k, v in waits:
                e.wait_ge(sems[k], v)
            if fw is not None and ins.fuse == 2:
                bi = ins.emit(e, (sems[fw[0]], fw[1]))
            else:
                bi = ins.emit(e)
                if fw is not None:
                    bi._wait_ge(sems[fw[0]], fw[1])
            k, v = ins.sig
            bi.then_inc(sems[k], 16 if ins.is_dma else 1)
        if eng == "sp":
            for k, v in self.final_waits.items():
                e.wait_ge(sems[k], v)


def _fd(ap):
    n = 1
    for d in ap.shape[1:]:
        n *= int(d)
    return n


ASET = {}
MODEL_DEEP = [0]
MODEL_PSUM = [0]
OPJ_OWN = [True]
TP_FIXED = [False]


def build_program(n_tiles=NTILE, do_sample=True, do_phase_a=True, n_pre=3 * NTILE, window=700, **_ignored):
    nc = bass.Bass("TRN2", target_bir_lowering=False)
    S = Sched()
    es = ExitStack()
    ASET.update({AF.Silu: "A", AF.Tanh: "A", AF.Ln: "B", AF.Exp: "B", AF.Gelu: "C", AF.Sigmoid: "D"})

    def dram_in(name, shape):
        return nc.dram_tensor(name, list(shape), F32, kind="ExternalInput").ap()

    def dram_out(name, shape):
        return nc.dram_tensor(name, list(shape), F32, kind="ExternalOutput").ap()

    xp_d = dram_in("xp", [SEG, D])
    xpre_d = dram_in("xpre", [3 * SEG, D])
    xs_d = dram_in("xs", [32, D])
    s0_d = dram_in("s0", [2, 4, 128, 128])
    vecs_d = dram_in("vecs", [28, 128])
    fnw_d = dram_in("fnw", [1, D])
    lnw_d = dram_in("lnwrow", [1, 512])
    lnb_d = dram_in("lnbrow", [1, 512])
    bs_d = dram_in("bsrow", [1, 512])
    ws_d = dram_in("ws", [4, 128, 128])
    win_d = dram_in("win", [D, INW])
    wout_d = dram_in("wout", [D, D])

    yp_d = dram_out("yp", [SEG, D])
    ys_d = dram_out("ys", [32, D])
    sp_d = dram_out("spo", [4, 128, 128])
    ss_d = dram_out("sso", [2, 4, 128, 128])
    vs_d = dram_out("vso", [32, 512])

    with es:
        def sb(name, shape, dt=F32):
            t = es.enter_context(nc.sbuf_tensor(name, list(shape), dt))
            return t, Buf(name)

        def sb4(name, shape, dt=F32):
            t = es.enter_context(nc.sbuf_tensor(name, list(shape), dt))
            return t, [Buf(f"{name}_h{h}") for h in range(4)]

        def ring(name, shape, dt=F32, depth=2):
            return [sb(f"{name}{j}", shape, dt) for j in range(depth)]

        def ring4(name, shape, dt=F32, depth=2):
            return [sb4(f"{name}{j}", shape, dt) for j in range(depth)]

        def ps(name, shape, dt=F32):
            t = es.enter_context(nc.psum_tensor(name, list(shape), dt))
            return t, Buf(name)

        Wi, _ = sb("Wi", [128, KT, INW], BF16)
        Wo, _ = sb("Wo", [128, KT, D], BF16)
        Wi_blk = [[] for j in range(7)]
        Wo_blk = [[] for j in range(2)]
        wst = [sb(f"wst{j}", [128, KT, 128], F32) for j in range(2)]

        vec_sb, vec_b = sb("vec_sb", [28, 128])
        vT, vT_b = sb("vT", [128, 28])
        lbc, lbc_b = sb("lbc", [128, 32])
        ident_f, ident_f_b = sb("ident_f", [128, 128])
        ident_bf, ident_bf_b = sb("ident_bf", [128, 128], BF16)
        mask01, mask01_b = sb("mask01", [128, 128])
        mask_s, mask_s_b = sb("mask_s", [48, 48])
        ones, ones_b = sb("ones", [128, 128])
        mhalf, mhalf_b = sb("mhalf", [128, 1])
        rmask, rmask_b = sb("rmask", [128, 512])
        rmask_s, rmask_s_b = sb("rmask_s", [128, 192])
        fnw_bc, fnw_bc_b = sb("fnw_bc", [128, D])
        lnw_bc, lnw_bc_b = sb("lnw_bc", [128, 512])
        lnb_bc, lnb_bc_b = sb("lnb_bc", [128, 512])
        bs_row, bs_row_b = sb("bs_row", [1, 512])
        bs2_row, bs2_row_b = sb("bs2_row", [1, 4, 48])
        wtmp, wtmp_b = sb("wtmp", [128, 128])
        wtmp2, wtmp2_b = sb("wtmp2", [48, 48])
        WsTf, WsTf_b = sb("WsTf", [128, 128])
        WsTsf, WsTsf_b = sb("WsTsf", [48, 48])
        WsT_bf, WsT_bf_b = sb("WsT_bf", [128, 4, 128], BF16)
        WsTs_bf, WsTs_bf_b = sb("WsTs_bf", [48, 4, 48], BF16)
        Bh, Bh_b = sb("Bh", [128, 4, 128])
        Bhs, Bhs_b = sb("Bhs", [128, 4, 48])

        R_xin = ring("xin", [128, D])
        junk_bf, junk_bf_b = sb("junk_bf", [128, D], BF16)
        junk_f, junk_f_b = sb("junk_f", [128, 128])
        R_xn = ring("xn_bf", [128, D], BF16)
        R_xnT = ring("xnT", [128, KT, 128], BF16)
        R_st1 = ring("st1", [128, 8], depth=4)
        R_st2 = ring("st2", [128, 8])
        R_tf = ring("t_f", [128, 512])
        R_logf = ring("logf", [128, 512])
        R_bcum = ring("bcum", [128, 512])
        R_enb = ring("enb", [128, 512])
        R_ebq = ring("ebq", [128, 512])
        R_sq = ring("sq", [128, 512])
        R_pk = ring("pk", [128, 2, 16], depth=4)
        R_epk = ring("epk", [128, 2, 16], depth=4)
        R_qT = ring("qT_bf", [128, 512], BF16)
        R_kkT = ring("kkT_bf", [128, 512], BF16)
        R_kk = ring("kk_bf", [128, 512], BF16)
        R_i = ring("i_bf", [128, 512], BF16)
        R_scT = ring("scT_bf", [128, 512], BF16)
        R_sga = ring("sga", [128, 512])
        R_gu = ring("gu", [128, 512])
        R_sgb = ring("sgb", [128, 512])
        R_gv = ring("gv", [128, 512])
        R_bnst = ring("bnst", [128, 8])
        vn, vn_b = sb("vn", [48, 512])
        R_vhat = ring("vhat_bf", [128, 512], BF16)
        R_ost = ring("ost", [128, 16])
        R_ya = ring4("ya_bf", [128, 512], BF16)
        R_yT = ring("yT_bf", [128, KT, 128], BF16)
        R_mix = ring4("mixtmp", [128, 512])
        R_hres = ring("hres", [128, D])

        Sp, Sp_b = sb4("Sp", [128, 512])
        Ss = [sb4(f"Ss{j}", [128, 512]) for j in range(2)]
        Smid_bf = [sb(f"Smid{j}", [128, 512], BF16) for j in range(2)]
        R_Sdec = ring("Sdec", [128, 512])
        sout, sout_b = sb("sout", [128, 512])

        ip = [ps(f"ip{j}", [128, 512]) for j in range(2)]
        tpr = [ps(f"tp{j}", [128, 1024], BF16) for j in range(2 if OPJ_OWN[0] else 3)]
        opj_own = ps("opj", [128, 512]) if OPJ_OWN[0] else None
        scp, scp_b = ps("scp", [128, 512])
        obp, obp_b = ps("obp", [128, 512])
        sup_g, sup_g_b = ps("sup", [128, 512])
        ip_rr = [0]
        tp_rr = [0]

        ip_pool = [ip]

        def next_ip():
            lst = ip_pool[0]
            j = ip_rr[0] % len(lst)
            ip_rr[0] += 1
            return lst[j]

        def next_tp(role=0):
            if TP_FIXED[0]:
                return tpr[0] if role == 0 else tpr[1]
            j = tp_rr[0] % len(tpr)
            tp_rr[0] += 1
            return tpr[j]

        def dma(out_ap, in_ap, reads, writes, key, final=False, nbytes=65536):
            def em(e):
                return e.dma_start(out=out_ap, in_=in_ap)
            S.op("sp", em, reads=reads, writes=writes, dma_key=key, final=final, dur=0.15,
                 lat=2.0 + nbytes / 150e3)

        def act(out_ap, in_ap, func, reads, writes, bias=None, scale=None, accum_out=None):
            nap = (0 if bias is None or isinstance(bias, float) else 1) + (0 if scale is None or isinstance(scale, float) else 1)

            def em(e):
                kw = {}
                if bias is not None:
                    kw["bias"] = bias
                if scale is not None:
                    kw["scale"] = scale
                if accum_out is not None:
                    kw["accum_out"] = accum_out
                return e.activation(out=out_ap, in_=in_ap, func=func, **kw)
            S.op("act", em, reads=reads, writes=writes, dur=0.2 + _fd(in_ap) / 1400.0 + 0.09 * nap + (0.1 if accum_out is not None else 0.0),
                 aset=ASET.get(func), fuse=(0 if accum_out is not None else 1))

        def veng(eng, fn, reads, writes, fd=128, kind="tt"):
            if eng == "pool":
                dur = 0.75 + fd / 480.0 if fd <= 16 else 0.45 + fd / 420.0
            elif kind == "stt" or kind == "scan":
                dur = 0.15 + fd / 480.0
            elif kind == "ts":
                dur = 0.15 + fd / 1900.0
            else:
                dur = 0.15 + fd / 960.0
            S.op(eng, fn, reads=reads, writes=writes, dur=dur, fuse=VENG_FUSE[0])

        def tt(eng, out_ap, a, b, op, reads, writes):
            veng(eng, lambda e: e.tensor_tensor(out=out_ap, in0=a, in1=b, op=op), reads, writes, _fd(out_ap))

        def ts(eng, out_ap, a, s1, s2, op0, op1, reads, writes):
            if s2 is None:
                if eng == "pool":
                    veng(eng, lambda e: e.tensor_scalar(out=out_ap, in0=a, scalar1=s1, scalar2=1.0, op0=op0, op1=ALU.mult),
                         reads, writes, _fd(out_ap))
                else:
                    veng(eng, lambda e: e.tensor_scalar(out=out_ap, in0=a, scalar1=s1, scalar2=None, op0=op0),
                         reads, writes, _fd(out_ap), "ts")
            else:
                veng(eng, lambda e: e.tensor_scalar(out=out_ap, in0=a, scalar1=s1, scalar2=s2, op0=op0, op1=op1),
                     reads, writes, _fd(out_ap))

        def stt(out_ap, a, scalar, b, op0, op1, reads, writes):
            veng("dve", lambda e: e.scalar_tensor_tensor(out=out_ap, in0=a, scalar=scalar, in1=b, op0=op0, op1=op1),
                 reads, writes, _fd(out_ap), "stt")

        def cp(eng, out_ap, in_ap, reads, writes):
            veng(eng, lambda e: e.tensor_copy(out=out_ap, in_=in_ap), reads, writes, _fd(out_ap))

        def mm_group(items, reads, writes):
            dur = 0.0
            for (o, l, r, st, sp_) in items:
                nfree = _fd(r)
                passes = 4 if r.dtype == F32 else 1
                dur += passes * (max(nfree, 64) * 0.00054 + 0.009)

            def em(e, w=None):
                bi = None
                for (o, l, r, st, sp_) in items:
                    bi = e.matmul(o, l, r, start=st, stop=sp_)
                    if w is not None:
                        bi._wait_ge(w[0], w[1])
                        w = None
                return bi
            S.op("pe", em, reads=reads, writes=writes, dur=dur, lat=dur + 0.1, fuse=2)

        def tr_group(items, reads, writes):
            dur = sum((max(_fd(i_), 64) * 0.00054 + 0.009) * (4 if i_.dtype == F32 else 1) for (o, i_, idn) in items)

            def em(e, w=None):
                bi = None
                for (o, i_, idn) in items:
                    bi = e.transpose(o, i_, idn)
                    if w is not None:
                        bi._wait_ge(w[0], w[1])
                        w = None
                return bi
            S.op("pe", em, reads=reads, writes=writes, dur=dur, lat=dur + 0.1, fuse=2)

        veng("pool", lambda e: e.memset(ident_f[:], 0.0), [], [ident_f_b])
        veng("pool", lambda e: e.affine_select(out=ident_f[:], in_=ident_f[:], pattern=[[-1, 128]],
                                               compare_op=ALU.not_equal, fill=1.0, base=0, channel_multiplier=1),
             [ident_f_b], [ident_f_b])
        cp("dve", ident_bf[:], ident_f[:], [ident_f_b], [ident_bf_b])
        veng("pool", lambda e: e.memset(mask01[:], 1.0), [], [mask01_b])
        veng("pool", lambda e: e.affine_select(out=mask01[:], in_=mask01[:], pattern=[[1, 128]],
                                               compare_op=ALU.is_ge, fill=0.0, base=0, channel_multiplier=-1),
             [mask01_b], [mask01_b])
        veng("pool", lambda e: e.memset(mask_s[:], 0.0), [], [mask_s_b])
        cp("pool", mask_s[0:16, 0:16], mask01[0:16, 0:16], [mask01_b], [mask_s_b])
        cp("pool", mask_s[32:48, 32:48], mask01[32:48, 32:48], [mask01_b], [mask_s_b])
        veng("pool", lambda e: e.memset(ones[:], 1.0), [], [ones_b])
        veng("pool", lambda e: e.memset(rmask[:], 1.0), [], [rmask_b])
        veng("pool", lambda e: e.memset(rmask[:, :].rearrange("p (h t) -> p h t", h=4)[:, :, 0:1], 0.0), [rmask_b], [rmask_b])
        veng("pool", lambda e: e.memset(rmask_s[:], 1.0), [], [rmask_s_b])
        veng("pool", lambda e: e.memset(rmask_s[:, :].rearrange("p (h t) -> p h t", h=4)[:, :, 0:1], 0.0), [rmask_s_b], [rmask_s_b])
        veng("pool", lambda e: e.memset(rmask_s[:, :].rearrange("p (h t) -> p h t", h=4)[:, :, 32:33], 0.0), [rmask_s_b], [rmask_s_b])
        veng("pool", lambda e: e.memset(mhalf[:], -0.5), [], [mhalf_b])
        veng("pool", lambda e: e.memset(wtmp2[:], 0.0), [], [wtmp2_b])
        veng("pool", lambda e: e.memset(bs2_row[:], 0.0), [], [bs2_row_b])
        veng("pool", lambda e: e.memset(Sp[:], 0.0), [], Sp_b)

        dma(vec_sb[:], vecs_d[:, :], [], [vec_b], "c0")
        dma(bs_row[:], bs_d[:, :], [], [bs_row_b], "c2")
        dma(fnw_bc[:], fnw_d.partition_broadcast(128), [], [fnw_bc_b], "c3")
        dma(lnw_bc[:], lnw_d.partition_broadcast(128), [], [lnw_bc_b], "c4")
        dma(lnb_bc[:], lnb_d.partition_broadcast(128), [], [lnb_bc_b], "c5")
        for h in range(4):
            dma(bs2_row[0:1, h, 0:16], bs_d[0:1, h * 128:h * 128 + 16], [], [bs2_row_b], "c6")
            dma(bs2_row[0:1, h, 32:48], bs_d[0:1, h * 128:h * 128 + 16], [], [bs2_row_b], "c6")

        ipt, ipt_b = ip[0]
        tr_group([(ipt[:, 0:28], vec_sb[0:28, :], ident_f[0:28, 0:28])], [vec_b, ident_f_b], [ipt_b])
        cp("dve", vT[:], ipt[:, 0:28], [ipt_b], [vT_b])
        tt("dve", lbc[:, 20:24], vT[:, 4:8], vT[:, 0:4], ALU.subtract, [vT_b], [lbc_b])
        act(lbc[:, 24:28], lbc[:, 20:24], AF.Sigmoid, [lbc_b], [lbc_b])
        ts("dve", lbc[:, 0:4], lbc[:, 24:28], 0.5, None, ALU.mult, None, [lbc_b], [lbc_b])
        ts("dve", lbc[:, 4:8], lbc[:, 0:4], -1.0, None, ALU.mult, None, [lbc_b], [lbc_b])
        ts("dve", lbc[:, 8:12], lbc[:, 0:4], -1.0, 1.0, ALU.mult, ALU.add, [lbc_b], [lbc_b])
        act(lbc[:, 12:16], lbc[:, 0:4], AF.Ln, [lbc_b], [lbc_b], scale=float(DK ** -0.5))
        veng("dve", lambda e: e.reciprocal(out=lbc[:, 16:20], in_=lbc[:, 0:4]), [lbc_b], [lbc_b])
        nBc = lambda h: lbc[:, 4 + h:5 + h]
        Acol = lambda h: lbc[:, 8 + h:9 + h]
        gnw = lambda kt: vT[:, 8 + kt:9 + kt]
        lnwc = lambda h: vT[:, 12 + h:13 + h]
        nwc = lambda kt: vT[:, 20 + kt:21 + kt]

        for h in range(4):
            dma(wtmp[:], ws_d[h, :, :], [], [wtmp_b], "c7")
            ipa, ipa_b = next_ip()
            tr_group([(ipa[:, 0:128], wtmp[:, :], ident_f[:, :])], [wtmp_b, ident_f_b], [ipa_b])
            tt("dve", WsTf[:], ipa[:, 0:128], mask01[:], ALU.mult, [ipa_b, mask01_b], [WsTf_b])
            cp("dve", WsT_bf[:, h, :], WsTf[:], [WsTf_b], [WsT_bf_b])
            ipb, ipb_b = next_ip()
            mm_group([(ipb[:, 0:128], lnb_bc[:, h * 128:(h + 1) * 128], WsTf[:], True, False),
                      (ipb[:, 0:128], ones[0:1, 0:128], bs_row[0:1, h * 128:(h + 1) * 128], False, True)],
                     [lnb_bc_b, WsTf_b, ones_b, bs_row_b], [ipb_b])
            cp("dve", Bh[:, h, :], ipb[:, 0:128], [ipb_b], [Bh_b])
            dma(wtmp2[0:16, 0:16], ws_d[h, 0:16, 0:16], [], [wtmp2_b], "c8")
            dma(wtmp2[32:48, 32:48], ws_d[h, 0:16, 0:16], [], [wtmp2_b], "c8")
            ipa, ipa_b = next_ip()
            tr_group([(ipa[0:48, 0:48], wtmp2[:, :], ident_f[0:48, 0:48])], [wtmp2_b, ident_f_b], [ipa_b])
            tt("dve", WsTsf[:], ipa[0:48, 0:48], mask_s[:], ALU.mult, [ipa_b, mask_s_b], [WsTsf_b])
            cp("dve", WsTs_bf[:, h, :], WsTsf[:], [WsTsf_b], [WsTs_bf_b])
            ipb, ipb_b = next_ip()
            mm_group([(ipb[:, 0:48], lnb_bc[0:48, h * 128:(h + 1) * 128], WsTsf[:], True, False),
                      (ipb[:, 0:48], ones[0:1, 0:128], bs2_row[0:1, h, :], False, True)],
                     [lnb_bc_b, WsTsf_b, ones_b, bs2_row_b], [ipb_b])
            cp("dve", Bhs[:, h, :], ipb[:, 0:48], [ipb_b], [Bhs_b])

        wctr = [0]
        wchunks = []
        WCH = 128

        def load_w_block(src_d, dst, dst_b, c0, width, scale_fn, extra_reads, defer=False):
            for cc in range(c0, c0 + width, WCH):
                if defer:
                    wchunks.append((src_d, dst, dst_b, cc, scale_fn, extra_reads))
                else:
                    load_w_chunk(src_d, dst, dst_b, cc, scale_fn, extra_reads)

        def load_w_chunk(src_d, dst, dst_b, cc, scale_fn, extra_reads):
            j = wctr[0] % 2
            wctr[0] += 1
            wt, wt_b = wst[j]
            dma(wt[:], src_d[:, cc:cc + WCH].rearrange("(kt p) c -> p kt c", p=128), [], [wt_b], f"w{j}",
                nbytes=128 * KT * WCH * 4)
            for kt in range(KT):
                eng = "pool" if (kt % 4 == 3) else "dve"
                q_ = 1 if eng == "pool" else 0
                sc = scale_fn(kt)
                wb = Buf(f"w_{cc}_{kt}")
                dst_b.append(wb)
                if sc is None:
                    cp(eng, dst[:, kt, cc:cc + WCH], wt[:, kt, :], [wt_b], [wb])
                else:
                    ts(eng, dst[:, kt, cc:cc + WCH], wt[:, kt, :], sc, None, ALU.mult, None,
                       [wt_b] + extra_reads, [wb])

        for blk in (1, 2):
            load_w_block(win_d, Wi, Wi_blk[blk], blk * 512, 512, nwc, [vT_b])

        tctr = [0]

        def tile_pass(x_src, TT, segs, states, full, WsT_use, Bh_use, out_dst, v_dst, rings):
            W4 = 4 * TT
            sup0, sup0_b = sup_g, sup_g_b
            sup, sup_b = sup_g, sup_g_b
            ti = tctr[0]
            tctr[0] += 1
            pick = lambda r: r[ti % len(r)]
            RR = rings
            xt, xt_b = pick(RR["xin"])
            xT, xT_b = pick(RR["xnT"])
            xn_bf, xn_bf_b = pick(R_xn)
            st1, st1_b = pick(R_st1)
            st2, st2_b = pick(R_st2)
            t_f, t_f_b = pick(RR["tf"])
            logf, logf_b = pick(RR["logf"])
            bcum, bcum_b = pick(RR["bcum"])
            enb, enb_b = pick(RR["enb"])
            ebq, ebq_b = pick(R_ebq)
            sq, sq_b = pick(R_sq)
            pk, pk_b = pick(R_pk)
            epk, epk_b = pick(R_epk)
            qT_bf, qT_bf_b = pick(R_qT)
            kkT_bf, kkT_bf_b = pick(RR["kkT"])
            kk_bf, kk_bf_b = pick(RR["kk"])
            i_bf, i_bf_b = pick(RR["i"])
            scT_bf, scT_bf_b = pick(R_scT)
            sga, sga_b = pick(R_sga)
            gu, gu_b = pick(R_gu)
            sgb, sgb_b = pick(R_sgb)
            gv, gv_b = pick(R_gv)
            bnst, bnst_b = pick(R_bnst)
            vhat_bf, vhat_bf_b = pick(R_vhat)
            ost, ost_b = pick(R_ost)
            ya_bf, ya_bf_b = pick(R_ya)
            yT_bf, yT_bf_b = pick(R_yT)
            mixtmp, mixtmp_b = pick(R_mix)
            hres, hres_b = pick(R_hres)
            Sdec, Sdec_b = pick(RR["Sdec"])
            xkey = "x_" + xt_b.name

            if TT == 128:
                dma(xt[:], x_src, [], [xt_b], xkey, nbytes=128 * D * 4)
            else:
                veng("pool", lambda e: e.memset(xt[0:TT, :], 0.0), [], [xt_b], D)
                dma(xt[0:16, :], x_src[0:16, :], [], [xt_b], xkey)
                dma(xt[32:48, :], x_src[16:32, :], [], [xt_b], xkey)
            act(xn_bf[0:TT, :], xt[0:TT, :], AF.Square, [xt_b], [st1_b, xn_bf_b], accum_out=st1[0:TT, 0:1])
            ts("dve", st1[0:TT, 1:2], st1[0:TT, 0:1], 1.0 / D, EPS, ALU.mult, ALU.add, [st1_b], [st1_b])
            tt("pool", st1[0:TT, 2:3], st1[0:TT, 1:2], mhalf[0:TT, :], ALU.pow, [st1_b, mhalf_b], [st1_b])
            ts("dve", xn_bf[0:TT, :], xt[0:TT, :], st1[0:TT, 2:3], None, ALU.mult, None, [xt_b, st1_b], [xn_bf_b])
            if full:
                tp, tp_b = next_tp(0)
            elif ti % 2 == 0:
                tp, tp_b = tpr[0]
            else:
                tp, tp_b = obp[:, :].bitcast(BF16), obp_b
            tr_group([(tp[:, kt * 128:kt * 128 + TT], xn_bf[0:TT, kt * 128:(kt + 1) * 128], ident_bf[0:TT, 0:TT])
                      for kt in range(KT)], [xn_bf_b, ident_bf_b], [tp_b])
            if full:
                cp("dve", xT[:, :, 0:TT], tp[:, :].rearrange("p (k t) -> p k t", k=KT)[:, :, 0:TT], [tp_b], [xT_b])
            else:
                act(xT[:, :, 0:TT], tp[:, :].rearrange("p (k t) -> p k t", k=KT)[:, :, 0:TT], AF.Copy, [tp_b], [xT_b])

            def proj_fm(c0, blk_b):
                bank, bank_b = next_ip()
                items = []
                for h in range(4):
                    for kt in range(KT):
                        items.append((bank[:, h * TT:(h + 1) * TT], Wi[:, kt, c0 + h * 128:c0 + (h + 1) * 128],
                                      xT[:, kt, 0:TT], kt == 0, kt == KT - 1))
                mm_group(items, [xT_b] + blk_b, [bank_b])
                return bank, bank_b

            def proj_tm(c0, blk_b):
                bank, bank_b = next_ip()
                items = [(bank[0:TT, :], xT[:, kt, 0:TT], Wi[:, kt, c0:c0 + 512], kt == 0, kt == KT - 1)
                         for kt in range(KT)]
                mm_group(items, [xT_b] + blk_b, [bank_b])
                return bank, bank_b

            if full:
                qb, qb_b = proj_fm(C_Q, Wi_blk[0])
                act(sq[:, 0:W4], qb[:, 0:W4], AF.Silu, [qb_b], [sq_b])
            fb, fb_b = proj_fm(C_F, Wi_blk[1])
            if full:
                act(t_f[:, 0:W4], fb[:, 0:W4], AF.Tanh, [fb_b], [t_f_b], scale=-0.5)
            else:
                act(t_f[:, 0:W4], fb[:, 0:W4], AF.Sigmoid, [fb_b], [t_f_b], scale=-1.0)
            if full:
                gab, gab_b = proj_tm(C_GA, Wi_blk[3])
                act(sga[0:TT, :], gab[0:TT, :], AF.Silu, [gab_b], [sga_b])
                gbb, gbb_b = proj_fm(C_GB, Wi_blk[6])
                act(sgb[:, 0:W4], gbb[:, 0:W4], AF.Silu, [gbb_b], [sgb_b])
            ib_, ib_b = proj_tm(C_I, Wi_blk[2])
            if full:
                cp("dve", i_bf[0:TT, :], ib_[0:TT, :], [ib_b], [i_bf_b])
            else:
                act(i_bf[0:TT, :], ib_[0:TT, :], AF.Copy, [ib_b], [i_bf_b])

            v4 = lambda t_: t_[:, 0:W4].rearrange("p (h t) -> p h t", h=4)
            if full:
                tt("pool", v4(logf), v4(t_f), lbc[:, 4:8].unsqueeze(2).to_broadcast([128, 4, TT]), ALU.mult,
                   [t_f_b, lbc_b], [logf_b])
                tt("pool", v4(logf), v4(logf), lbc[:, 8:12].unsqueeze(2).to_broadcast([128, 4, TT]), ALU.add,
                   [logf_b, lbc_b], [logf_b])
                act(logf[:, 0:W4], logf[:, 0:W4], AF.Ln, [logf_b], [logf_b])
            else:
                tt("pool", v4(logf), v4(t_f), lbc[:, 24:28].unsqueeze(2).to_broadcast([128, 4, TT]), ALU.mult,
                   [t_f_b, lbc_b], [logf_b])
                act(logf[:, 0:W4], logf[:, 0:W4], AF.Ln, [logf_b], [logf_b], scale=-1.0, bias=1.0)
            bc3 = v4(bcum)
            rm = rmask if TT == 128 else rmask_s
            rm_b = rmask_b if TT == 128 else rmask_s_b
            veng("dve", lambda e: e.tensor_tensor_scan(
                out=bcum[:, 0:W4], data0=rm[:, 0:W4], data1=logf[:, 0:W4], initial=0.0, op0=ALU.mult, op1=ALU.add),
                [logf_b, rm_b], [bcum_b], W4, "scan")
            for si, (r0, n) in enumerate(segs):
                mid = r0 + n // 2 - 1
                last = r0 + n - 1
                if not full:
                    cp("dve", pk[:, si, 4:8], bc3[:, :, last], [bcum_b], [pk_b])
                    act(epk[:, si, 4:8], pk[:, si, 4:8], AF.Exp, [pk_b], [epk_b])
                    tt("pool", v4(enb)[:, :, r0:r0 + n], bc3[:, :, r0:r0 + n],
                       pk[:, si, 4:8].unsqueeze(2).to_broadcast([128, 4, n]), ALU.subtract, [bcum_b, pk_b], [enb_b])
                    continue
                cp("dve", pk[:, si, 0:4], bc3[:, :, mid], [bcum_b], [pk_b])
                cp("dve", pk[:, si, 4:8], bc3[:, :, last], [bcum_b], [pk_b])
                tt("dve", pk[:, si, 8:12], pk[:, si, 4:8], pk[:, si, 0:4], ALU.subtract, [pk_b], [pk_b])
                tt("dve", pk[:, si, 12:16], lbc[:, 12:16], pk[:, si, 0:4], ALU.subtract, [pk_b, lbc_b], [pk_b])
                act(epk[:, si, 0:12], pk[:, si, 0:12], AF.Exp, [pk_b], [epk_b])
                tt("pool", v4(enb)[:, :, r0:r0 + n], bc3[:, :, r0:r0 + n],
                   pk[:, si, 0:4].unsqueeze(2).to_broadcast([128, 4, n]), ALU.subtract, [bcum_b, pk_b], [enb_b])
                tt("pool", v4(ebq)[:, :, r0:r0 + n], bc3[:, :, r0:r0 + n],
                   pk[:, si, 12:16].unsqueeze(2).to_broadcast([128, 4, n]), ALU.add, [bcum_b, pk_b], [ebq_b])
            if len(segs) > 1:
                veng("pool", lambda e: e.memset(v4(enb)[:, :, 16:32], 0.0), [enb_b], [enb_b], 64)
                if full:
                    veng("pool", lambda e: e.memset(v4(ebq)[:, :, 16:32], 0.0), [ebq_b], [ebq_b], 64)
            act(enb[:, 0:W4], enb[:, 0:W4], AF.Exp, [enb_b], [enb_b], scale=-1.0)
            if full:
                act(ebq[:, 0:W4], ebq[:, 0:W4], AF.Exp, [ebq_b], [ebq_b])
            if full:
                stt(kkT_bf[:, 0:W4], t_f[:, 0:W4], 1.0, enb[:, 0:W4], ALU.add, ALU.mult, [t_f_b, enb_b], [kkT_bf_b])
            else:
                stt(kkT_bf[:, 0:W4], t_f[:, 0:W4], 2.0, enb[:, 0:W4], ALU.mult, ALU.mult, [t_f_b, enb_b], [kkT_bf_b])
            if full:
                tt("dve", qT_bf[:, 0:W4], sq[:, 0:W4], ebq[:, 0:W4], ALU.mult, [sq_b, ebq_b], [qT_bf_b])
            if full:
                tp, tp_b = next_tp(1)
            else:
                tp, tp_b = tpr[1]
            tr_group([(tp[0:TT, h * 128:(h + 1) * 128], kkT_bf[:, h * TT:(h + 1) * TT], ident_bf[:, :])
                      for h in range(4)], [kkT_bf_b, ident_bf_b], [tp_b])
            cp("dve", kk_bf[0:TT, :], tp[0:TT, 0:512], [tp_b], [kk_bf_b])

            for si, (r0, n) in enumerate(segs):
                St, St_b, Sm, Sm_b = states[si]
                if full:
                    veng("pool", lambda e, St=St, Sm=Sm, si=si: e.tensor_tensor(
                        out=Sm[:, :].rearrange("p (h v) -> p h v", h=4),
                        in0=St[:, :].rearrange("p (h v) -> p h v", h=4),
                        in1=epk[:, si, 0:4].unsqueeze(2).to_broadcast([128, 4, 128]), op=ALU.mult),
                        St_b + [epk_b], [Sm_b], 512)
            if full:
                items = []
                for h in range(4):
                    items.append((scp[0:TT, h * TT:(h + 1) * TT], kkT_bf[:, h * TT:(h + 1) * TT],
                                  qT_bf[:, h * TT:(h + 1) * TT], True, True))
                mm_group(items, [kkT_bf_b, qT_bf_b], [scp_b])
                msk = mask01 if TT == 128 else mask_s
                msk_b = mask01_b if TT == 128 else mask_s_b
                veng("dve", lambda e: e.tensor_tensor(
                    out=scT_bf[0:TT, 0:W4].rearrange("p (h t) -> p h t", h=4),
                    in0=scp[0:TT, 0:W4].rearrange("p (h t) -> p h t", h=4),
                    in1=msk[0:TT, 0:TT].unsqueeze(1).to_broadcast([TT, 4, TT]), op=ALU.mult),
                    [scp_b, msk_b], [scT_bf_b], W4)
                items = []
                smr = []
                for h in range(4):
                    for si, (r0, n) in enumerate(segs):
                        St, St_b, Sm, Sm_b = states[si]
                        items.append((obp[r0:r0 + n, h * 128:(h + 1) * 128],
                                      scT_bf[0:TT, h * TT + r0:h * TT + r0 + n],
                                      i_bf[0:TT, h * 128:(h + 1) * 128], True, False))
                        items.append((obp[r0:r0 + n, h * 128:(h + 1) * 128], qT_bf[:, h * TT + r0:h * TT + r0 + n],
                                      Sm[:, h * 128:(h + 1) * 128], False, True))
                        smr.append(Sm_b)
                mm_group(items, [scT_bf_b, i_bf_b, qT_bf_b] + smr, [obp_b])
            for si, (r0, n) in enumerate(segs):
                St, St_b, Sm, Sm_b = states[si]
                if False and (not full) and (ti % 2 == 1):
                    sup, sup_b = opj_own
                else:
                    sup, sup_b = sup0, sup0_b
                items = [(sup[:, h * 128:(h + 1) * 128], kk_bf[r0:r0 + n, h * 128:(h + 1) * 128],
                          i_bf[r0:r0 + n, h * 128:(h + 1) * 128], True, True) for h in range(4)]
                mm_group(items, [kk_bf_b, i_bf_b], [sup_b])
                if full:
                    veng("pool", lambda e, St=St, si=si: e.tensor_tensor(
                        out=Sdec[:, :].rearrange("p (h v) -> p h v", h=4),
                        in0=St[:, :].rearrange("p (h v) -> p h v", h=4),
                        in1=epk[:, si, 4:8].unsqueeze(2).to_broadcast([128, 4, 128]), op=ALU.mult),
                        St_b + [epk_b], [Sdec_b], 512)
                if full:
                    for h in range(4):
                        stt(St[:, h * 128:(h + 1) * 128], sup[:, h * 128:(h + 1) * 128], epk[:, si, 8 + h:9 + h],
                            Sdec[:, h * 128:(h + 1) * 128], ALU.mult, ALU.add, [sup_b, epk_b, Sdec_b], [St_b[h]])
                else:
                    for h in range(4):
                        stt(St[:, h * 128:(h + 1) * 128], St[:, h * 128:(h + 1) * 128], epk[:, si, 4 + h:5 + h],
                            sup[:, h * 128:(h + 1) * 128], ALU.mult, ALU.add, [sup_b, epk_b, St_b[h]], [St_b[h]])
            if not full:
                return

            for h in range(4):
                act(ya_bf[0:TT, h * 128:(h + 1) * 128], obp[0:TT, h * 128:(h + 1) * 128], AF.Square, [obp_b],
                    [ost_b, ya_bf_b[h]], accum_out=ost[0:TT, h:h + 1])
            ts("dve", ost[0:TT, 4:8], ost[0:TT, 0:4], 1.0 / 128, EPS, ALU.mult, ALU.add, [ost_b], [ost_b])
            tt("pool", ost[0:TT, 8:12], ost[0:TT, 4:8], mhalf[0:TT, 0:1].to_broadcast([TT, 4]), ALU.pow,
               [ost_b, mhalf_b], [ost_b])
            for h in range(4):
                stt(ya_bf[0:TT, h * 128:(h + 1) * 128], obp[0:TT, h * 128:(h + 1) * 128], ost[0:TT, 8 + h:9 + h],
                    sga[0:TT, h * 128:(h + 1) * 128], ALU.mult, ALU.mult, [obp_b, ost_b, sga_b], [ya_bf_b[h]])
            tp, tp_b = next_tp(1)
            tr_group([(tp[:, h * 128:h * 128 + TT], ya_bf[0:TT, h * 128:(h + 1) * 128], ident_bf[0:TT, 0:TT])
                      for h in range(4)], ya_bf_b + [ident_bf_b], [tp_b])
            cp("dve", yT_bf[:, 0:4, 0:TT], tp[:, 0:512].rearrange("p (k t) -> p k t", k=4)[:, :, 0:TT],
               [tp_b], [yT_bf_b])

            vb, vb_b = proj_tm(C_V, Wi_blk[5])
            act(gv[0:TT, :], vb[0:TT, :], AF.Gelu, [vb_b], [gv_b])
            ub, ub_b = proj_fm(C_U, Wi_blk[4])
            act(gu[:, 0:W4], ub[:, 0:W4], AF.Gelu, [ub_b], [gu_b])
            veng("dve", lambda e: e.bn_stats(out=bnst[0:TT, 0:6], in_=gv[0:TT, :]), [gv_b], [bnst_b], 512)
            veng("dve", lambda e: e.bn_aggr(out=bnst[0:TT, 6:8], in_=bnst[0:TT, 0:6]), [bnst_b], [bnst_b])
            ts("dve", st2[0:TT, 4:5], bnst[0:TT, 7:8], EPS, None, ALU.add, None, [bnst_b], [st2_b])
            tt("pool", st2[0:TT, 5:6], st2[0:TT, 4:5], mhalf[0:TT, :], ALU.pow, [st2_b, mhalf_b], [st2_b])
            ts("dve", vhat_bf[0:TT, :], gv[0:TT, :], bnst[0:TT, 6:7], st2[0:TT, 5:6], ALU.subtract, ALU.mult,
               [gv_b, bnst_b, st2_b], [vhat_bf_b])
            if v_dst is not None:
                ts("dve", vn[0:TT, :], gv[0:TT, :], bnst[0:TT, 6:7], st2[0:TT, 5:6], ALU.subtract, ALU.mult,
                   [gv_b, bnst_b, st2_b], [vn_b])
                tt("pool", vn[0:TT, :], vn[0:TT, :], lnw_bc[0:TT, :], ALU.mult, [vn_b, lnw_bc_b], [vn_b])
                tt("pool", vn[0:TT, :], vn[0:TT, :], lnb_bc[0:TT, :], ALU.add, [vn_b, lnb_bc_b], [vn_b])
                dma(v_dst[0:16, :], vn[0:16, :], [vn_b], [], "vs", final=True)
                dma(v_dst[16:32, :], vn[32:48, :], [vn_b], [], "vs", final=True)
            items = [(sup[:, h * TT:(h + 1) * TT], vhat_bf[0:TT, h * 128:(h + 1) * 128], WsT_use[0:TT, h, 0:TT],
                      True, True) for h in range(4)]
            mm_group(items, [vhat_bf_b, WsT_bf_b, WsTs_bf_b], [sup_b])
            for h in range(4):
                stt(mixtmp[:, h * TT:(h + 1) * TT], sup[:, h * TT:(h + 1) * TT], lnwc(h), Bh_use[:, h, 0:TT],
                    ALU.mult, ALU.add, [sup_b, vT_b, Bh_b, Bhs_b], [mixtmp_b[h]])
            tt("pool", mixtmp[:, 0:W4], mixtmp[:, 0:W4], gu[:, 0:W4], ALU.mult, mixtmp_b + [gu_b], mixtmp_b)
            veng("dve", lambda e: e.tensor_tensor(
                out=yT_bf[:, 4:8, 0:TT], in0=mixtmp[:, 0:W4].rearrange("p (h t) -> p h t", h=4),
                in1=sgb[:, 0:W4].rearrange("p (h t) -> p h t", h=4), op=ALU.mult),
                mixtmp_b + [sgb_b], [yT_bf_b], W4)

            for half in range(2):
                opj, opj_b = (opj_own if OPJ_OWN[0] else next_ip())
                items = [(opj[0:TT, :], yT_bf[:, kt, 0:TT], Wo[:, kt, half * 512:(half + 1) * 512],
                          kt == 0, kt == KT - 1) for kt in range(KT)]
                mm_group(items, [yT_bf_b] + Wo_blk[half], [opj_b])
                tt("dve", hres[0:TT, half * 512:(half + 1) * 512], opj[0:TT, :], xt[0:TT, half * 512:(half + 1) * 512],
                   ALU.add, [opj_b, xt_b], [hres_b])
            act(xn_bf[0:TT, :], hres[0:TT, :], AF.Square, [hres_b], [st2_b, xn_bf_b], accum_out=st2[0:TT, 0:1])
            ts("dve", st2[0:TT, 1:2], st2[0:TT, 0:1], 1.0 / D, EPS, ALU.mult, ALU.add, [st2_b], [st2_b])
            tt("pool", st2[0:TT, 2:3], st2[0:TT, 1:2], mhalf[0:TT, :], ALU.pow, [st2_b, mhalf_b], [st2_b])
            stt(hres[0:TT, :], hres[0:TT, :], st2[0:TT, 2:3], fnw_bc[0:TT, :], ALU.mult, ALU.mult,
                [hres_b, st2_b, fnw_bc_b], [hres_b])
            if TT == 128:
                dma(out_dst, hres[:, :], [hres_b], [], "y_" + hres_b.name, final=True, nbytes=128 * D * 4)
            else:
                dma(out_dst[0:16, :], hres[0:16, :], [hres_b], [], "y_" + hres_b.name, final=True)
                dma(out_dst[16:32, :], hres[32:48, :], [hres_b], [], "y_" + hres_b.name, final=True)

        prompt_state = [(Sp, Sp_b, Smid_bf[0][0], Smid_bf[0][1])]
        R_xin_b = R_xin + [(wst[0][0][:, :, :].rearrange("p k c -> p (k c)"), wst[0][1]),
                           (wst[1][0][:, :, :].rearrange("p k c -> p (k c)"), wst[1][1])]
        RB = {"xin": R_xin_b, "xnT": R_xnT, "tf": R_tf, "logf": R_logf, "bcum": R_bcum, "enb": R_enb,
              "kkT": R_kkT, "kk": R_kk, "i": R_i, "Sdec": R_Sdec}
        if MODEL_DEEP[0] > 0:
            nc2 = bass.Bass("TRN2", target_bir_lowering=False)
            if MODEL_PSUM[0]:
                for j in range(4):
                    tpr.append((es.enter_context(nc2.psum_tensor(f"tpx{j}", [128, 1024], BF16)), Buf(f"tpx{j}")))

            def extra(name, shape, dt, k):
                return [(es.enter_context(nc2.sbuf_tensor(f"{name}_x{j}", list(shape), dt)), Buf(f"{name}_x{j}")) for j in range(k)]
            k_ = MODEL_DEEP[0]
            R_hres = R_hres + extra("xin", [128, D], F32, k_)
            R_yT = R_yT + extra("xnT", [128, KT, 128], BF16, k_)
            R_sq = R_sq + extra("tf", [128, 512], F32, k_)
            R_ebq = R_ebq + extra("logf", [128, 512], F32, k_)
            R_sga = R_sga + extra("bcum", [128, 512], F32, k_)
            R_gu = R_gu + extra("enb", [128, 512], F32, k_)
            R_qT = R_qT + extra("kkT", [128, 512], BF16, k_)
            R_scT = R_scT + extra("kk", [128, 512], BF16, k_)
            R_vhat = R_vhat + extra("i", [128, 512], BF16, k_)
            R_xn.extend(extra("xn", [128, D], BF16, k_))
            R_st1.extend(extra("st1", [128, 8], F32, k_))
            R_pk.extend(extra("pk", [128, 2, 16], F32, k_))
            R_epk.extend(extra("epk", [128, 2, 16], F32, k_))
        RA = {"xin": R_xin + R_hres, "xnT": R_xnT + R_yT, "tf": R_tf + R_sq, "logf": R_logf + R_ebq,
              "bcum": R_bcum + R_sga, "enb": R_enb + R_gu, "kkT": R_kkT + R_qT, "kk": R_kk + R_scT,
              "i": R_i + R_vhat, "Sdec": R_Sdec + R_sgb}

        for blk in (0, 3, 6, 5, 4):
            load_w_block(win_d, Wi, Wi_blk[blk], blk * 512, 512, nwc, [vT_b], defer=True)
        for half in range(2):
            load_w_block(wout_d, Wo, Wo_blk[half], half * 512, 512, lambda kt: (gnw(kt) if kt < 4 else None), [vT_b],
                         defer=True)
        if do_phase_a:
            ip_pool[0] = ip + [(scp, scp_b), opj_own]
            for t in range(n_pre):
                tile_pass(xpre_d[t * 128:(t + 1) * 128, :], 128, [(0, 128)], prompt_state, False,
                          None, None, None, None, RA)
                if wchunks:
                    load_w_chunk(*wchunks.pop(0))
        while wchunks:
            load_w_chunk(*wchunks.pop(0))
        ip_pool[0] = ip

        for t in range(n_tiles):
            tile_pass(xp_d[t * 128:(t + 1) * 128, :], 128, [(0, 128)], prompt_state, True,
                      WsT_bf, Bh, yp_d[t * 128:(t + 1) * 128, :], None, RB)
        veng("dve", lambda e: e.tensor_tensor(
            out=sout[:, :].rearrange("p (h v) -> p h v", h=4),
            in0=Sp[:, :].rearrange("p (h v) -> p h v", h=4),
            in1=lbc[:, 0:4].unsqueeze(2).to_broadcast([128, 4, 128]), op=ALU.mult),
            Sp_b + [lbc_b], [sout_b], 512)
        dma(sp_d.rearrange("h k v -> k h v"), sout[:, :].rearrange("p (h v) -> p h v", h=4), [sout_b], [], "spo",
            final=True, nbytes=262144)

        if do_sample:
            sst = []
            for q_ in range(2):
                St, St_b = Ss[q_]
                dma(St[:, :].rearrange("p (h v) -> p h v", h=4), s0_d[q_].rearrange("h k v -> k h v"), [], St_b,
                    f"s0{q_}", nbytes=262144)
                veng("dve", lambda e, St=St: e.tensor_tensor(
                    out=St[:, :].rearrange("p (h v) -> p h v", h=4),
                    in0=St[:, :].rearrange("p (h v) -> p h v", h=4),
                    in1=lbc[:, 16:20].unsqueeze(2).to_broadcast([128, 4, 128]), op=ALU.mult),
                    St_b + [lbc_b], St_b, 512)
                sst.append((St, St_b, Smid_bf[q_][0], Smid_bf[q_][1]))
            tile_pass(xs_d, 48, [(0, 16), (32, 16)], sst, True, WsTs_bf, Bhs, ys_d, vs_d, RB)
            for q_ in range(2):
                St, St_b = Ss[q_]
                veng("dve", lambda e, St=St: e.tensor_tensor(
                    out=sout[:, :].rearrange("p (h v) -> p h v", h=4),
                    in0=St[:, :].rearrange("p (h v) -> p h v", h=4),
                    in1=lbc[:, 0:4].unsqueeze(2).to_broadcast([128, 4, 128]), op=ALU.mult),
                    St_b + [lbc_b], [sout_b], 512)
                dma(ss_d[q_].rearrange("h k v -> k h v"), sout[:, :].rearrange("p (h v) -> p h v", h=4),
                    [sout_b], [], "sso", final=True, nbytes=262144)

        build_program.sbuf_free = nc.sbuf_bytes_remaining
        S.schedule(window=window)
        build_program.last_sched = S
        if MODEL_DEEP[0] > 0:
            return None
        sems = {k: es.enter_context(nc.semaphore("s_" + k)) for k in S.sem_keys()}
        with nc.allow_non_contiguous_dma(reason="small strided constant / state layouts"):
            with nc.Block() as block:
                @block.sync
                def _(e):
                    S.emit("sp", e, sems)

                @block.tensor
                def _(e):
                    S.emit("pe", e, sems)

                @block.scalar
                def _(e):
                    S.emit("act", e, sems)

                @block.vector
                def _(e):
                    S.emit("dve", e, sems)

                @block.gpsimd
                def _(e):
                    S.emit("pool", e, sems)
    build_program.last_sched = S
    return nc


_NC_CACHE = {}


def _get_nc():
    if "nc" not in _NC_CACHE:
        _NC_CACHE["nc"] = build_program()
    return _NC_CACHE["nc"]


def make_in_maps(x_prompt, x_sample, state_hgrn, norm_w, w_in, lb_logits, g_norm_w, ln_v_w, ln_v_b,
                 w_s, b_s, w_out, final_norm_w):
    f = lambda a: np.ascontiguousarray(np.asarray(a, dtype=np.float32))
    x_prompt, x_sample, state_hgrn = f(x_prompt), f(x_sample), f(state_hgrn)
    vecs = np.concatenate([
        f(lb_logits).reshape(8, 128), f(g_norm_w).reshape(4, 128), f(ln_v_w).reshape(4, 128),
        f(ln_v_b).reshape(4, 128), f(norm_w).reshape(8, 128)], axis=0)
    shared = {
        "vecs": f(vecs),
        "fnw": f(final_norm_w).reshape(1, D),
        "lnwrow": f(ln_v_w).reshape(1, 512),
        "lnbrow": f(ln_v_b).reshape(1, 512),
        "bsrow": f(b_s).reshape(1, 512),
        "ws": f(w_s).reshape(4, 128, 128),
        "win": f(w_in).reshape(D, INW),
        "wout": f(w_out).reshape(D, D),
    }
    in_maps = []
    for c in range(NCORES):
        b, s = c // 4, c % 4
        m = dict(shared)
        m["xp"] = f(x_prompt[b, s * SEG:(s + 1) * SEG, :])
        pre = np.zeros((3 * SEG, D), np.float32)
        if s > 0:
            pre[(3 - s) * SEG:, :] = x_prompt[b, 0:s * SEG, :]
        m["xpre"] = pre
        m["xs"] = f(x_sample[2 * c:2 * c + 2].reshape(32, D))
        m["s0"] = f(state_hgrn[0, 2 * c:2 * c + 2])
        in_maps.append(m)
    return in_maps


def kernel(x_prompt, x_sample, state_hgrn, norm_w, w_in, lb_logits, g_norm_w, ln_v_w, ln_v_b,
           w_s, b_s, w_out, final_norm_w):
    in_maps = make_in_maps(x_prompt, x_sample, state_hgrn, norm_w, w_in, lb_logits, g_norm_w, ln_v_w, ln_v_b,
                           w_s, b_s, w_out, final_norm_w)
    nc = _get_nc()
    res = run_bass_kernel_spmd(nc, in_maps, core_ids=list(range(NCORES)))
    r = res.results
    y_prompt = np.zeros((2, 8192, D), np.float32)
    y_sample = np.zeros((16, 16, D), np.float32)
    sp = np.zeros((1, 2, 4, 128, 128), np.float32)
    ss = np.zeros((1, 16, 4, 128, 128), np.float32)
    vs = np.zeros((1, 16, 16, 512), np.float32)
    for c in range(NCORES):
        b, s = c // 4, c % 4
        y_prompt[b, s * SEG:(s + 1) * SEG, :] = r[c]["yp"]
        y_sample[2 * c:2 * c + 2] = r[c]["ys"].reshape(2, 16, D)
        ss[0, 2 * c:2 * c + 2] = r[c]["sso"]
        vs[0, 2 * c:2 * c + 2] = r[c]["vso"].reshape(2, 16, 512)
        if s == 3:
            sp[0, b] = r[c]["spo"]
    return (y_prompt, y_sample, sp, ss, vs)
```

```python
import numpy as np
from contextlib import ExitStack

import concourse.bass as bass
import concourse.mybir as mybir
from concourse.bass_utils import run_bass_kernel_spmd

F32 = mybir.dt.float32
BF16 = mybir.dt.bfloat16
AF = mybir.ActivationFunctionType
ALU = mybir.AluOpType

NCORES = 8
D = 1024
KT = 8
INW = 3584
SEG = 2048
NTILE = SEG // 128
EPS = 1e-6
DK = 128
C_Q, C_F, C_I, C_GA, C_U, C_V, C_GB = 0, 512, 1024, 1536, 2048, 2560, 3072


class Buf:
    def __init__(self, name):
        self.name = name
        self.last_w = None
        self.readers = []


class Ins:
    __slots__ = ("idx", "eng", "emit", "deps", "waits", "sig", "clock", "is_dma", "key", "dur", "lat", "aset",
                 "final", "start", "finish", "nsucc", "succs", "npred", "fuse")


ACT_SWITCH_US = 1.3
ACT_SWITCH_DECISION_US = [1.3]
HOP_US = 1.0
FUSE_WAITS = [True]
VENG_FUSE = [1]


class Sched:
    ENGS = ("pe", "act", "dve", "pool", "sp")

    def __init__(self):
        self.ops = []
        self.final_waits = {}

    def op(self, eng, emit, reads=(), writes=(), dma_key=None, final=False, dur=0.3, lat=None, aset=None, fuse=0):
        ins = Ins()
        ins.fuse = fuse if FUSE_WAITS[0] else 0
        ins.idx = len(self.ops)
        ins.eng = eng
        ins.emit = emit
        ins.is_dma = dma_key is not None
        ins.key = dma_key if ins.is_dma else eng
        ins.dur = dur
        ins.lat = dur if lat is None else lat
        ins.aset = aset
        ins.final = final
        deps = {}
        for b in reads:
            if b.last_w is not None and b.last_w is not ins:
                deps[b.last_w.idx] = (b.last_w, "RAW")
        for b in writes:
            if b.last_w is not None and b.last_w is not ins:
                if b.last_w.idx not in deps:
                    deps[b.last_w.idx] = (b.last_w, "WAW")
            for r in b.readers:
                if r is not ins and r.idx not in deps:
                    deps[r.idx] = (r, "WAR")
        ins.deps = list(deps.values())
        for b in writes:
            b.last_w = ins
            b.readers = []
        for b in reads:
            if b.last_w is not ins:
                b.readers.append(ins)
        self.ops.append(ins)
        return ins

    def schedule(self, window=700):
        ops = self.ops
        n = len(ops)
        for o in ops:
            o.succs = []
            o.npred = len(o.deps)
            o.start = None
        for o in ops:
            for d, _ in o.deps:
                d.succs.append(o)
        free = {e: 0.0 for e in self.ENGS}
        cur_set = [None]
        avail = set(o.idx for o in ops if o.npred == 0)
        done = 0
        lo = 0
        order = []
        self.q = {e: [] for e in self.ENGS}
        while done < n:
            while lo < n and ops[lo].start is not None:
                lo += 1
            best = None
            best_t = None
            for i in avail:
                if i >= lo + window:
                    continue
                o = ops[i]
                rdy = 0.0
                for d, kind in o.deps:
                    f = d.finish
                    if d.eng != o.eng or d.is_dma or o.is_dma:
                        f += HOP_US
                    if f > rdy:
                        rdy = f
                t = max(rdy, free[o.eng])
                tdec = t
                if o.eng == "act" and o.aset is not None and cur_set[0] is not None and o.aset != cur_set[0]:
                    t += ACT_SWITCH_US
                    tdec = t - ACT_SWITCH_US + ACT_SWITCH_DECISION_US[0]
                if best is None or tdec < best_d - 1e-9 or (abs(tdec - best_d) <= 1e-9 and i < best.idx):
                    best, best_t, best_d = o, t, tdec
            o = best
            avail.discard(o.idx)
            if o.eng == "act" and o.aset is not None:
                cur_set[0] = o.aset
            o.start = best_t
            free[o.eng] = best_t + o.dur
            o.finish = best_t + o.lat
            order.append(o)
            self.q[o.eng].append(o)
            done += 1
            for s_ in o.succs:
                s_.npred -= 1
                if s_.npred == 0:
                    avail.add(s_.idx)
        self.makespan = max(o.finish for o in ops)
        known = {e: {} for e in self.ENGS}
        cnt = {}
        for o in order:
            kn = known[o.eng]
            waits = {}
            for d, kind in sorted(o.deps, key=lambda x: x[0].start):
                if d.eng == o.eng and (not d.is_dma) and (not o.is_dma) and o.eng == "pe":
                    continue
                k, v = d.sig
                if kn.get(k, 0) >= v:
                    continue
                waits[k] = max(waits.get(k, 0), v)
                for kk, vv in d.clock.items():
                    if kn.get(kk, 0) < vv:
                        kn[kk] = vv
            o.waits = sorted(waits.items())
            inc = 16 if o.is_dma else 1
            cnt[o.key] = cnt.get(o.key, 0) + inc
            o.sig = (o.key, cnt[o.key])
            o.clock = dict(kn)
            o.clock[o.key] = cnt[o.key]
            if o.final:
                self.final_waits[o.key] = max(self.final_waits.get(o.key, 0), cnt[o.key])
        self.cnt = cnt

    def sem_keys(self):
        return list(self.cnt.keys())

    def emit(self, eng, e, sems):
        for ins in self.q[eng]:
            waits = list(ins.waits)
            fw = None
            if ins.fuse and waits:
                fw = waits.pop()
            for k, v in waits:
                e.wait_ge(sems[k], v)
            if fw is not None and ins.fuse == 2:
                bi = ins.emit(e, (sems[fw[0]], fw[1]))
            else:
                bi = ins.emit(e)
                if fw is not None:
                    bi._wait_ge(sems[fw[0]], fw[1])
            k, v = ins.sig
            bi.then_inc(sems[k], 16 if ins.is_dma else 1)
        if eng == "sp":
            for k, v in self.final_waits.items():
                e.wait_ge(sems[k], v)


def _fd(ap):
    n = 1
    for d in ap.shape[1:]:
        n *= int(d)
    return n


ASET = {}
MODEL_DEEP = [0]
MODEL_PSUM = [0]
OPJ_OWN = [True]
TP_FIXED = [False]


def build_program(n_tiles=NTILE, do_sample=True, do_phase_a=True, n_pre=3 * NTILE, window=700, **_ignored):
    nc = bass.Bass("TRN2", target_bir_lowering=False)
    S = Sched()
    es = ExitStack()
    ASET.update({AF.Silu: "A", AF.Tanh: "A", AF.Ln: "B", AF.Exp: "B", AF.Gelu: "C", AF.Sigmoid: "D"})

    def dram_in(name, shape):
        return nc.dram_tensor(name, list(shape), F32, kind="ExternalInput").ap()

    def dram_out(name, shape):
        return nc.dram_tensor(name, list(shape), F32, kind="ExternalOutput").ap()

    xp_d = dram_in("xp", [SEG, D])
    xpre_d = dram_in("xpre", [3 * SEG, D])
    xs_d = dram_in("xs", [32, D])
    s0_d = dram_in("s0", [2, 4, 128, 128])
    vecs_d = dram_in("vecs", [28, 128])
    fnw_d = dram_in("fnw", [1, D])
    lnw_d = dram_in("lnwrow", [1, 512])
    lnb_d = dram_in("lnbrow", [1, 512])
    bs_d = dram_in("bsrow", [1, 512])
    ws_d = dram_in("ws", [4, 128, 128])
    win_d = dram_in("win", [D, INW])
    wout_d = dram_in("wout", [D, D])

    yp_d = dram_out("yp", [SEG, D])
    ys_d = dram_out("ys", [32, D])
    sp_d = dram_out("spo", [4, 128, 128])
    ss_d = dram_out("sso", [2, 4, 128, 128])
    vs_d = dram_out("vso", [32, 512])

    with es:
        def sb(name, shape, dt=F32):
            t = es.enter_context(nc.sbuf_tensor(name, list(shape), dt))
            return t, Buf(name)

        def sb4(name, shape, dt=F32):
            t = es.enter_context(nc.sbuf_tensor(name, list(shape), dt))
            return t, [Buf(f"{name}_h{h}") for h in range(4)]

        def ring(name, shape, dt=F32, depth=2):
            return [sb(f"{name}{j}", shape, dt) for j in range(depth)]

        def ring4(name, shape, dt=F32, depth=2):
            return [sb4(f"{name}{j}", shape, dt) for j in range(depth)]

        def ps(name, shape, dt=F32):
            t = es.enter_context(nc.psum_tensor(name, list(shape), dt))
            return t, Buf(name)

        Wi, _ = sb("Wi", [128, KT, INW], BF16)
        Wo, _ = sb("Wo", [128, KT, D], BF16)
        Wi_blk = [[] for j in range(7)]
        Wo_blk = [[] for j in range(2)]
        wst = [sb(f"wst{j}", [128, KT, 128], F32) for j in range(2)]

        vec_sb, vec_b = sb("vec_sb", [28, 128])
        vT, vT_b = sb("vT", [128, 28])
        lbc, lbc_b = sb("lbc", [128, 32])
        ident_f, ident_f_b = sb("ident_f", [128, 128])
        ident_bf, ident_bf_b = sb("ident_bf", [128, 128], BF16)
        mask01, mask01_b = sb("mask01", [128, 128])
        mask_s, mask_s_b = sb("mask_s", [48, 48])
        ones, ones_b = sb("ones", [128, 128])
        mhalf, mhalf_b = sb("mhalf", [128, 1])
        rmask, rmask_b = sb("rmask", [128, 512])
        rmask_s, rmask_s_b = sb("rmask_s", [128, 192])
        fnw_bc, fnw_bc_b = sb("fnw_bc", [128, D])
        lnw_bc, lnw_bc_b = sb("lnw_bc", [128, 512])
        lnb_bc, lnb_bc_b = sb("lnb_bc", [128, 512])
        bs_row, bs_row_b = sb("bs_row", [1, 512])
        bs2_row, bs2_row_b = sb("bs2_row", [1, 4, 48])
        wtmp, wtmp_b = sb("wtmp", [128, 128])
        wtmp2, wtmp2_b = sb("wtmp2", [48, 48])
        WsTf, WsTf_b = sb("WsTf", [128, 128])
        WsTsf, WsTsf_b = sb("WsTsf", [48, 48])
        WsT_bf, WsT_bf_b = sb("WsT_bf", [128, 4, 128], BF16)
        WsTs_bf, WsTs_bf_b = sb("WsTs_bf", [48, 4, 48], BF16)
        Bh, Bh_b = sb("Bh", [128, 4, 128])
        Bhs, Bhs_b = sb("Bhs", [128, 4, 48])

        R_xin = ring("xin", [128, D])
        junk_bf, junk_bf_b = sb("junk_bf", [128, D], BF16)
        junk_f, junk_f_b = sb("junk_f", [128, 128])
        R_xn = ring("xn_bf", [128, D], BF16)
        R_xnT = ring("xnT", [128, KT, 128], BF16)
        R_st1 = ring("st1", [128, 8], depth=4)
        R_st2 = ring("st2", [128, 8])
        R_tf = ring("t_f", [128, 512])
        R_logf = ring("logf", [128, 512])
        R_bcum = ring("bcum", [128, 512])
        R_enb = ring("enb", [128, 512])
        R_ebq = ring("ebq", [128, 512])
        R_sq = ring("sq", [128, 512])
        R_pk = ring("pk", [128, 2, 16], depth=4)
        R_epk = ring("epk", [128, 2, 16], depth=4)
        R_qT = ring("qT_bf", [128, 512], BF16)
        R_kkT = ring("kkT_bf", [128, 512], BF16)
        R_kk = ring("kk_bf", [128, 512], BF16)
        R_i = ring("i_bf", [128, 512], BF16)
        R_scT = ring("scT_bf", [128, 512], BF16)
        R_sga = ring("sga", [128, 512])
        R_gu = ring("gu", [128, 512])
        R_sgb = ring("sgb", [128, 512])
        R_gv = ring("gv", [128, 512])
        R_bnst = ring("bnst", [128, 8])
        vn, vn_b = sb("vn", [48, 512])
        R_vhat = ring("vhat_bf", [128, 512], BF16)
        R_ost = ring("ost", [128, 16])
        R_ya = ring4("ya_bf", [128, 512], BF16)
        R_yT = ring("yT_bf", [128, KT, 128], BF16)
        R_mix = ring4("mixtmp", [128, 512])
        R_hres = ring("hres", [128, D])

        Sp, Sp_b = sb4("Sp", [128, 512])
        Ss = [sb4(f"Ss{j}", [128, 512]) for j in range(2)]
        Smid_bf = [sb(f"Smid{j}", [128, 512], BF16) for j in range(2)]
        R_Sdec = ring("Sdec", [128, 512])
        sout, sout_b = sb("sout", [128, 512])

        ip = [ps(f"ip{j}", [128, 512]) for j in range(2)]
        tpr = [ps(f"tp{j}", [128, 1024], BF16) for j in range(2 if OPJ_OWN[0] else 3)]
        opj_own = ps("opj", [128, 512]) if OPJ_OWN[0] else None
        scp, scp_b = ps("scp", [128, 512])
        obp, obp_b = ps("obp", [128, 512])
        sup_g, sup_g_b = ps("sup", [128, 512])
        ip_rr = [0]
        tp_rr = [0]

        ip_pool = [ip]

        def next_ip():
            lst = ip_pool[0]
            j = ip_rr[0] % len(lst)
            ip_rr[0] += 1
            return lst[j]

        def next_tp(role=0):
            if TP_FIXED[0]:
                return tpr[0] if role == 0 else tpr[1]
            j = tp_rr[0] % len(tpr)
            tp_rr[0] += 1
            return tpr[j]

        def dma(out_ap, in_ap, reads, writes, key, final=False, nbytes=65536):
            def em(e):
                return e.dma_start(out=out_ap, in_=in_ap)
            S.op("sp", em, reads=reads, writes=writes, dma_key=key, final=final, dur=0.15,
                 lat=2.0 + nbytes / 150e3)

        def act(out_ap, in_ap, func, reads, writes, bias=None, scale=None, accum_out=None):
            nap = (0 if bias is None or isinstance(bias, float) else 1) + (0 if scale is None or isinstance(scale, float) else 1)

            def em(e):
                kw = {}
                if bias is not None:
                    kw["bias"] = bias
                if scale is not None:
                    kw["scale"] = scale
                if accum_out is not None:
                    kw["accum_out"] = accum_out
                return e.activation(out=out_ap, in_=in_ap, func=func, **kw)
            S.op("act", em, reads=reads, writes=writes, dur=0.2 + _fd(in_ap) / 1400.0 + 0.09 * nap + (0.1 if accum_out is not None else 0.0),
                 aset=ASET.get(func), fuse=(0 if accum_out is not None else 1))

        def veng(eng, fn, reads, writes, fd=128, kind="tt"):
            if eng == "pool":
                dur = 0.75 + fd / 480.0 if fd <= 16 else 0.45 + fd / 420.0
            elif kind == "stt" or kind == "scan":
                dur = 0.15 + fd / 480.0
            elif kind == "ts":
                dur = 0.15 + fd / 1900.0
            else:
                dur = 0.15 + fd / 960.0
            S.op(eng, fn, reads=reads, writes=writes, dur=dur, fuse=VENG_FUSE[0])

        def tt(eng, out_ap, a, b, op, reads, writes):
            veng(eng, lambda e: e.tensor_tensor(out=out_ap, in0=a, in1=b, op=op), reads, writes, _fd(out_ap))

        def ts(eng, out_ap, a, s1, s2, op0, op1, reads, writes):
            if s2 is None:
                if eng == "pool":
                    veng(eng, lambda e: e.tensor_scalar(out=out_ap, in0=a, scalar1=s1, scalar2=1.0, op0=op0, op1=ALU.mult),
                         reads, writes, _fd(out_ap))
                else:
                    veng(eng, lambda e: e.tensor_scalar(out=out_ap, in0=a, scalar1=s1, scalar2=None, op0=op0),
                         reads, writes, _fd(out_ap), "ts")
            else:
                veng(eng, lambda e: e.tensor_scalar(out=out_ap, in0=a, scalar1=s1, scalar2=s2, op0=op0, op1=op1),
                     reads, writes, _fd(out_ap))

        def stt(out_ap, a, scalar, b, op0, op1, reads, writes):
            veng("dve", lambda e: e.scalar_tensor_tensor(out=out_ap, in0=a, scalar=scalar, in1=b, op0=op0, op1=op1),
                 reads, writes, _fd(out_ap), "stt")

        def cp(eng, out_ap, in_ap, reads, writes):
            veng(eng, lambda e: e.tensor_copy(out=out_ap, in_=in_ap), reads, writes, _fd(out_ap))

        def mm_group(items, reads, writes):
            dur = 0.0
            for (o, l, r, st, sp_) in items:
                nfree = _fd(r)
                passes = 4 if r.dtype == F32 else 1
                dur += passes * (max(nfree, 64) * 0.00054 + 0.009)

            def em(e, w=None):
                bi = None
                for (o, l, r, st, sp_) in items:
                    bi = e.matmul(o, l, r, start=st, stop=sp_)
                    if w is not None:
                        bi._wait_ge(w[0], w[1])
                        w = None
                return bi
            S.op("pe", em, reads=reads, writes=writes, dur=dur, lat=dur + 0.1, fuse=2)

        def tr_group(items, reads, writes):
            dur = sum((max(_fd(i_), 64) * 0.00054 + 0.009) * (4 if i_.dtype == F32 else 1) for (o, i_, idn) in items)

            def em(e, w=None):
                bi = None
                for (o, i_, idn) in items:
                    bi = e.transpose(o, i_, idn)
                    if w is not None:
                        bi._wait_ge(w[0], w[1])
                        w = None
                return bi
            S.op("pe", em, reads=reads, writes=writes, dur=dur, lat=dur + 0.1, fuse=2)

        veng("pool", lambda e: e.memset(ident_f[:], 0.0), [], [ident_f_b])
        veng("pool", lambda e: e.affine_select(out=ident_f[:], in_=ident_f[:], pattern=[[-1, 128]],
                                               compare_op=ALU.not_equal, fill=1.0, base=0, channel_multiplier=1),
             [ident_f_b], [ident_f_b])
        cp("dve", ident_bf[:], ident_f[:], [ident_f_b], [ident_bf_b])
        veng("pool", lambda e: e.memset(mask01[:], 1.0), [], [mask01_b])
        veng("pool", lambda e: e.affine_select(out=mask01[:], in_=mask01[:], pattern=[[1, 128]],
                                               compare_op=ALU.is_ge, fill=0.0, base=0, channel_multiplier=-1),
             [mask01_b], [mask01_b])
        veng("pool", lambda e: e.memset(mask_s[:], 0.0), [], [mask_s_b])
        cp("pool", mask_s[0:16, 0:16], mask01[0:16, 0:16], [mask01_b], [mask_s_b])
        cp("pool", mask_s[32:48, 32:48], mask01[32:48, 32:48], [mask01_b], [mask_s_b])
        veng("pool", lambda e: e.memset(ones[:], 1.0), [], [ones_b])
        veng("pool", lambda e: e.memset(rmask[:], 1.0), [], [rmask_b])
        veng("pool", lambda e: e.memset(rmask[:, :].rearrange("p (h t) -> p h t", h=4)[:, :, 0:1], 0.0), [rmask_b], [rmask_b])
        veng("pool", lambda e: e.memset(rmask_s[:], 1.0), [], [rmask_s_b])
        veng("pool", lambda e: e.memset(rmask_s[:, :].rearrange("p (h t) -> p h t", h=4)[:, :, 0:1], 0.0), [rmask_s_b], [rmask_s_b])
        veng("pool", lambda e: e.memset(rmask_s[:, :].rearrange("p (h t) -> p h t", h=4)[:, :, 32:33], 0.0), [rmask_s_b], [rmask_s_b])
        veng("pool", lambda e: e.memset(mhalf[:], -0.5), [], [mhalf_b])
        veng("pool", lambda e: e.memset(wtmp2[:], 0.0), [], [wtmp2_b])
        veng("pool", lambda e: e.memset(bs2_row[:], 0.0), [], [bs2_row_b])
        veng("pool", lambda e: e.memset(Sp[:], 0.0), [], Sp_b)

        dma(vec_sb[:], vecs_d[:, :], [], [vec_b], "c0")
        dma(bs_row[:], bs_d[:, :], [], [bs_row_b], "c2")
        dma(fnw_bc[:], fnw_d.partition_broadcast(128), [], [fnw_bc_b], "c3")
        dma(lnw_bc[:], lnw_d.partition_broadcast(128), [], [lnw_bc_b], "c4")
        dma(lnb_bc[:], lnb_d.partition_broadcast(128), [], [lnb_bc_b], "c5")
        for h in range(4):
            dma(bs2_row[0:1, h, 0:16], bs_d[0:1, h * 128:h * 128 + 16], [], [bs2_row_b], "c6")
            dma(bs2_row[0:1, h, 32:48], bs_d[0:1, h * 128:h * 128 + 16], [], [bs2_row_b], "c6")

        ipt, ipt_b = ip[0]
        tr_group([(ipt[:, 0:28], vec_sb[0:28, :], ident_f[0:28, 0:28])], [vec_b, ident_f_b], [ipt_b])
        cp("dve", vT[:], ipt[:, 0:28], [ipt_b], [vT_b])
        tt("dve", lbc[:, 20:24], vT[:, 4:8], vT[:, 0:4], ALU.subtract, [vT_b], [lbc_b])
        act(lbc[:, 24:28], lbc[:, 20:24], AF.Sigmoid, [lbc_b], [lbc_b])
        ts("dve", lbc[:, 0:4], lbc[:, 24:28], 0.5, None, ALU.mult, None, [lbc_b], [lbc_b])
        ts("dve", lbc[:, 4:8], lbc[:, 0:4], -1.0, None, ALU.mult, None, [lbc_b], [lbc_b])
        ts("dve", lbc[:, 8:12], lbc[:, 0:4], -1.0, 1.0, ALU.mult, ALU.add, [lbc_b], [lbc_b])
        act(lbc[:, 12:16], lbc[:, 0:4], AF.Ln, [lbc_b], [lbc_b], scale=float(DK ** -0.5))
        veng("dve", lambda e: e.reciprocal(out=lbc[:, 16:20], in_=lbc[:, 0:4]), [lbc_b], [lbc_b])
        nBc = lambda h: lbc[:, 4 + h:5 + h]
        Acol = lambda h: lbc[:, 8 + h:9 + h]
        gnw = lambda kt: vT[:, 8 + kt:9 + kt]
        lnwc = lambda h: vT[:, 12 + h:13 + h]
        nwc = lambda kt: vT[:, 20 + kt:21 + kt]

        for h in range(4):
            dma(wtmp[:], ws_d[h, :, :], [], [wtmp_b], "c7")
            ipa, ipa_b = next_ip()
            tr_group([(ipa[:, 0:128], wtmp[:, :], ident_f[:, :])], [wtmp_b, ident_f_b], [ipa_b])
            tt("dve", WsTf[:], ipa[:, 0:128], mask01[:], ALU.mult, [ipa_b, mask01_b], [WsTf_b])
            cp("dve", WsT_bf[:, h, :], WsTf[:], [WsTf_b], [WsT_bf_b])
            ipb, ipb_b = next_ip()
            mm_group([(ipb[:, 0:128], lnb_bc[:, h * 128:(h + 1) * 128], WsTf[:], True, False),
                      (ipb[:, 0:128], ones[0:1, 0:128], bs_row[0:1, h * 128:(h + 1) * 128], False, True)],
                     [lnb_bc_b, WsTf_b, ones_b, bs_row_b], [ipb_b])
            cp("dve", Bh[:, h, :], ipb[:, 0:128], [ipb_b], [Bh_b])
            dma(wtmp2[0:16, 0:16], ws_d[h, 0:16, 0:16], [], [wtmp2_b], "c8")
            dma(wtmp2[32:48, 32:48], ws_d[h, 0:16, 0:16], [], [wtmp2_b], "c8")
            ipa, ipa_b = next_ip()
            tr_group([(ipa[0:48, 0:48], wtmp2[:, :], ident_f[0:48, 0:48])], [wtmp2_b, ident_f_b], [ipa_b])
            tt("dve", WsTsf[:], ipa[0:48, 0:48], mask_s[:], ALU.mult, [ipa_b, mask_s_b], [WsTsf_b])
            cp("dve", WsTs_bf[:, h, :], WsTsf[:], [WsTsf_b], [WsTs_bf_b])
            ipb, ipb_b = next_ip()
            mm_group([(ipb[:, 0:48], lnb_bc[0:48, h * 128:(h + 1) * 128], WsTsf[:], True, False),
                      (ipb[:, 0:48], ones[0:1, 0:128], bs2_row[0:1, h, :], False, True)],
                     [lnb_bc_b, WsTsf_b, ones_b, bs2_row_b], [ipb_b])
            cp("dve", Bhs[:, h, :], ipb[:, 0:48], [ipb_b], [Bhs_b])

        wctr = [0]
        wchunks = []
        WCH = 128

        def load_w_block(src_d, dst, dst_b, c0, width, scale_fn, extra_reads, defer=False):
            for cc in range(c0, c0 + width, WCH):
                if defer:
                    wchunks.append((src_d, dst, dst_b, cc, scale_fn, extra_reads))
                else:
                    load_w_chunk(src_d, dst, dst_b, cc, scale_fn, extra_reads)

        def load_w_chunk(src_d, dst, dst_b, cc, scale_fn, extra_reads):
            j = wctr[0] % 2
            wctr[0] += 1
            wt, wt_b = wst[j]
            dma(wt[:], src_d[:, cc:cc + WCH].rearrange("(kt p) c -> p kt c", p=128), [], [wt_b], f"w{j}",
                nbytes=128 * KT * WCH * 4)
            for kt in range(KT):
                eng = "pool" if (kt % 4 == 3) else "dve"
                q_ = 1 if eng == "pool" else 0
                sc = scale_fn(kt)
                wb = Buf(f"w_{cc}_{kt}")
                dst_b.append(wb)
                if sc is None:
                    cp(eng, dst[:, kt, cc:cc + WCH], wt[:, kt, :], [wt_b], [wb])
                else:
                    ts(eng, dst[:, kt, cc:cc + WCH], wt[:, kt, :], sc, None, ALU.mult, None,
                       [wt_b] + extra_reads, [wb])

        for blk in (1, 2):
            load_w_block(win_d, Wi, Wi_blk[blk], blk * 512, 512, nwc, [vT_b])

        tctr = [0]

        def tile_pass(x_src, TT, segs, states, full, WsT_use, Bh_use, out_dst, v_dst, rings):
            W4 = 4 * TT
            sup0, sup0_b = sup_g, sup_g_b
            sup, sup_b = sup_g, sup_g_b
            ti = tctr[0]
            tctr[0] += 1
            pick = lambda r: r[ti % len(r)]
            RR = rings
            xt, xt_b = pick(RR["xin"])
            xT, xT_b = pick(RR["xnT"])
            xn_bf, xn_bf_b = pick(R_xn)
            st1, st1_b = pick(R_st1)
            st2, st2_b = pick(R_st2)
            t_f, t_f_b = pick(RR["tf"])
            logf, logf_b = pick(RR["logf"])
            bcum, bcum_b = pick(RR["bcum"])
            enb, enb_b = pick(RR["enb"])
            ebq, ebq_b = pick(R_ebq)
            sq, sq_b = pick(R_sq)
            pk, pk_b = pick(R_pk)
            epk, epk_b = pick(R_epk)
            qT_bf, qT_bf_b = pick(R_qT)
            kkT_bf, kkT_bf_b = pick(RR["kkT"])
            kk_bf, kk_bf_b = pick(RR["kk"])
            i_bf, i_bf_b = pick(RR["i"])
            scT_bf, scT_bf_b = pick(R_scT)
            sga, sga_b = pick(R_sga)
            gu, gu_b = pick(R_gu)
            sgb, sgb_b = pick(R_sgb)
            gv, gv_b = pick(R_gv)
            bnst, bnst_b = pick(R_bnst)
            vhat_bf, vhat_bf_b = pick(R_vhat)
            ost, ost_b = pick(R_ost)
            ya_bf, ya_bf_b = pick(R_ya)
            yT_bf, yT_bf_b = pick(R_yT)
            mixtmp, mixtmp_b = pick(R_mix)
            hres, hres_b = pick(R_hres)
            Sdec, Sdec_b = pick(RR["Sdec"])
            xkey = "x_" + xt_b.name

            if TT == 128:
                dma(xt[:], x_src, [], [xt_b], xkey, nbytes=128 * D * 4)
            else:
                veng("pool", lambda e: e.memset(xt[0:TT, :], 0.0), [], [xt_b], D)
                dma(xt[0:16, :], x_src[0:16, :], [], [xt_b], xkey)
                dma(xt[32:48, :], x_src[16:32, :], [], [xt_b], xkey)
            act(xn_bf[0:TT, :], xt[0:TT, :], AF.Square, [xt_b], [st1_b, xn_bf_b], accum_out=st1[0:TT, 0:1])
            ts("pool", st1[0:TT, 1:2], st1[0:TT, 0:1], 1.0 / D, EPS, ALU.mult, ALU.add, [st1_b], [st1_b])
            tt("pool", st1[0:TT, 2:3], st1[0:TT, 1:2], mhalf[0:TT, :], ALU.pow, [st1_b, mhalf_b], [st1_b])
            ts("dve", xn_bf[0:TT, :], xt[0:TT, :], st1[0:TT, 2:3], None, ALU.mult, None, [xt_b, st1_b], [xn_bf_b])
            if full:
                tp, tp_b = next_tp(0)
            elif ti % 2 == 0:
                tp, tp_b = tpr[0]
            else:
                tp, tp_b = obp[:, :].bitcast(BF16), obp_b
            tr_group([(tp[:, kt * 128:kt * 128 + TT], xn_bf[0:TT, kt * 128:(kt + 1) * 128], ident_bf[0:TT, 0:TT])
                      for kt in range(KT)], [xn_bf_b, ident_bf_b], [tp_b])
            if full:
                cp("dve", xT[:, :, 0:TT], tp[:, :].rearrange("p (k t) -> p k t", k=KT)[:, :, 0:TT], [tp_b], [xT_b])
            else:
                act(xT[:, :, 0:TT], tp[:, :].rearrange("p (k t) -> p k t", k=KT)[:, :, 0:TT], AF.Copy, [tp_b], [xT_b])

            def proj_fm(c0, blk_b):
                bank, bank_b = next_ip()
                items = []
                for h in range(4):
                    for kt in range(KT):
                        items.append((bank[:, h * TT:(h + 1) * TT], Wi[:, kt, c0 + h * 128:c0 + (h + 1) * 128],
                                      xT[:, kt, 0:TT], kt == 0, kt == KT - 1))
                mm_group(items, [xT_b] + blk_b, [bank_b])
                return bank, bank_b

            def proj_tm(c0, blk_b):
                bank, bank_b = next_ip()
                items = [(bank[0:TT, :], xT[:, kt, 0:TT], Wi[:, kt, c0:c0 + 512], kt == 0, kt == KT - 1)
                         for kt in range(KT)]
                mm_group(items, [xT_b] + blk_b, [bank_b])
                return bank, bank_b

            if full:
                qb, qb_b = proj_fm(C_Q, Wi_blk[0])
                act(sq[:, 0:W4], qb[:, 0:W4], AF.Silu, [qb_b], [sq_b])
            fb, fb_b = proj_fm(C_F, Wi_blk[1])
            if full:
                act(t_f[:, 0:W4], fb[:, 0:W4], AF.Tanh, [fb_b], [t_f_b], scale=-0.5)
            else:
                act(t_f[:, 0:W4], fb[:, 0:W4], AF.Sigmoid, [fb_b], [t_f_b], scale=-1.0)
            if full:
                gab, gab_b = proj_tm(C_GA, Wi_blk[3])
                act(sga[0:TT, :], gab[0:TT, :], AF.Silu, [gab_b], [sga_b])
                gbb, gbb_b = proj_fm(C_GB, Wi_blk[6])
                act(sgb[:, 0:W4], gbb[:, 0:W4], AF.Silu, [gbb_b], [sgb_b])
            ib_, ib_b = proj_tm(C_I, Wi_blk[2])
            if full:
                cp("dve", i_bf[0:TT, :], ib_[0:TT, :], [ib_b], [i_bf_b])
            else:
                act(i_bf[0:TT, :], ib_[0:TT, :], AF.Copy, [ib_b], [i_bf_b])

            v4 = lambda t_: t_[:, 0:W4].rearrange("p (h t) -> p h t", h=4)
            if full:
                tt("pool", v4(logf), v4(t_f), lbc[:, 4:8].unsqueeze(2).to_broadcast([128, 4, TT]), ALU.mult,
                   [t_f_b, lbc_b], [logf_b])
                tt("pool", v4(logf), v4(logf), lbc[:, 8:12].unsqueeze(2).to_broadcast([128, 4, TT]), ALU.add,
                   [logf_b, lbc_b], [logf_b])
                act(logf[:, 0:W4], logf[:, 0:W4], AF.Ln, [logf_b], [logf_b])
            else:
                tt("pool", v4(logf), v4(t_f), lbc[:, 24:28].unsqueeze(2).to_broadcast([128, 4, TT]), ALU.mult,
                   [t_f_b, lbc_b], [logf_b])
                act(logf[:, 0:W4], logf[:, 0:W4], AF.Ln, [logf_b], [logf_b], scale=-1.0, bias=1.0)
            bc3 = v4(bcum)
            rm = rmask if TT == 128 else rmask_s
            rm_b = rmask_b if TT == 128 else rmask_s_b
            veng("dve", lambda e: e.tensor_tensor_scan(
                out=bcum[:, 0:W4], data0=rm[:, 0:W4], data1=logf[:, 0:W4], initial=0.0, op0=ALU.mult, op1=ALU.add),
                [logf_b, rm_b], [bcum_b], W4, "scan")
            for si, (r0, n) in enumerate(segs):
                mid = r0 + n // 2 - 1
                last = r0 + n - 1
                if not full:
                    cp("dve", pk[:, si, 4:8], bc3[:, :, last], [bcum_b], [pk_b])
                    act(epk[:, si, 4:8], pk[:, si, 4:8], AF.Exp, [pk_b], [epk_b])
                    tt("pool", v4(enb)[:, :, r0:r0 + n], bc3[:, :, r0:r0 + n],
                       pk[:, si, 4:8].unsqueeze(2).to_broadcast([128, 4, n]), ALU.subtract, [bcum_b, pk_b], [enb_b])
                    continue
                cp("dve", pk[:, si, 0:4], bc3[:, :, mid], [bcum_b], [pk_b])
                cp("dve", pk[:, si, 4:8], bc3[:, :, last], [bcum_b], [pk_b])
                tt("dve", pk[:, si, 8:12], pk[:, si, 4:8], pk[:, si, 0:4], ALU.subtract, [pk_b], [pk_b])
                tt("dve", pk[:, si, 12:16], lbc[:, 12:16], pk[:, si, 0:4], ALU.subtract, [pk_b, lbc_b], [pk_b])
                act(epk[:, si, 0:12], pk[:, si, 0:12], AF.Exp, [pk_b], [epk_b])
                tt("pool", v4(enb)[:, :, r0:r0 + n], bc3[:, :, r0:r0 + n],
                   pk[:, si, 0:4].unsqueeze(2).to_broadcast([128, 4, n]), ALU.subtract, [bcum_b, pk_b], [enb_b])
                tt("pool", v4(ebq)[:, :, r0:r0 + n], bc3[:, :, r0:r0 + n],
                   pk[:, si, 12:16].unsqueeze(2).to_broadcast([128, 4, n]), ALU.add, [bcum_b, pk_b], [ebq_b])
            if len(segs) > 1:
                veng("pool", lambda e: e.memset(v4(enb)[:, :, 16:32], 0.0), [enb_b], [enb_b], 64)
                if full:
                    veng("pool", lambda e: e.memset(v4(ebq)[:, :, 16:32], 0.0), [ebq_b], [ebq_b], 64)
            act(enb[:, 0:W4], enb[:, 0:W4], AF.Exp, [enb_b], [enb_b], scale=-1.0)
            if full:
                act(ebq[:, 0:W4], ebq[:, 0:W4], AF.Exp, [ebq_b], [ebq_b])
            if full:
                stt(kkT_bf[:, 0:W4], t_f[:, 0:W4], 1.0, enb[:, 0:W4], ALU.add, ALU.mult, [t_f_b, enb_b], [kkT_bf_b])
            else:
                stt(kkT_bf[:, 0:W4], t_f[:, 0:W4], 2.0, enb[:, 0:W4], ALU.mult, ALU.mult, [t_f_b, enb_b], [kkT_bf_b])
            if full:
                tt("dve", qT_bf[:, 0:W4], sq[:, 0:W4], ebq[:, 0:W4], ALU.mult, [sq_b, ebq_b], [qT_bf_b])
            if full:
                tp, tp_b = next_tp(1)
            else:
                tp, tp_b = tpr[1]
            tr_group([(tp[0:TT, h * 128:(h + 1) * 128], kkT_bf[:, h * TT:(h + 1) * TT], ident_bf[:, :])
                      for h in range(4)], [kkT_bf_b, ident_bf_b], [tp_b])
            cp("dve", kk_bf[0:TT, :], tp[0:TT, 0:512], [tp_b], [kk_bf_b])

            for si, (r0, n) in enumerate(segs):
                St, St_b, Sm, Sm_b = states[si]
                if full:
                    veng("pool", lambda e, St=St, Sm=Sm, si=si: e.tensor_tensor(
                        out=Sm[:, :].rearrange("p (h v) -> p h v", h=4),
                        in0=St[:, :].rearrange("p (h v) -> p h v", h=4),
                        in1=epk[:, si, 0:4].unsqueeze(2).to_broadcast([128, 4, 128]), op=ALU.mult),
                        St_b + [epk_b], [Sm_b], 512)
            if full:
                items = []
                for h in range(4):
                    items.append((scp[0:TT, h * TT:(h + 1) * TT], kkT_bf[:, h * TT:(h + 1) * TT],
                                  qT_bf[:, h * TT:(h + 1) * TT], True, True))
                mm_group(items, [kkT_bf_b, qT_bf_b], [scp_b])
                msk = mask01 if TT == 128 else mask_s
                msk_b = mask01_b if TT == 128 else mask_s_b
                veng("dve", lambda e: e.tensor_tensor(
                    out=scT_bf[0:TT, 0:W4].rearrange("p (h t) -> p h t", h=4),
                    in0=scp[0:TT, 0:W4].rearrange("p (h t) -> p h t", h=4),
                    in1=msk[0:TT, 0:TT].unsqueeze(1).to_broadcast([TT, 4, TT]), op=ALU.mult),
                    [scp_b, msk_b], [scT_bf_b], W4)
                items = []
                smr = []
                for h in range(4):
                    for si, (r0, n) in enumerate(segs):
                        St, St_b, Sm, Sm_b = states[si]
                        items.append((obp[r0:r0 + n, h * 128:(h + 1) * 128],
                                      scT_bf[0:TT, h * TT + r0:h * TT + r0 + n],
                                      i_bf[0:TT, h * 128:(h + 1) * 128], True, False))
                        items.append((obp[r0:r0 + n, h * 128:(h + 1) * 128], qT_bf[:, h * TT + r0:h * TT + r0 + n],
                                      Sm[:, h * 128:(h + 1) * 128], False, True))
                        smr.append(Sm_b)
                mm_group(items, [scT_bf_b, i_bf_b, qT_bf_b] + smr, [obp_b])
            for si, (r0, n) in enumerate(segs):
                St, St_b, Sm, Sm_b = states[si]
                if False and (not full) and (ti % 2 == 1):
                    sup, sup_b = opj_own
                else:
                    sup, sup_b = sup0, sup0_b
                items = [(sup[:, h * 128:(h + 1) * 128], kk_bf[r0:r0 + n, h * 128:(h + 1) * 128],
                          i_bf[r0:r0 + n, h * 128:(h + 1) * 128], True, True) for h in range(4)]
                mm_group(items, [kk_bf_b, i_bf_b], [sup_b])
                if full:
                    veng("pool", lambda e, St=St, si=si: e.tensor_tensor(
                        out=Sdec[:, :].rearrange("p (h v) -> p h v", h=4),
                        in0=St[:, :].rearrange("p (h v) -> p h v", h=4),
                        in1=epk[:, si, 4:8].unsqueeze(2).to_broadcast([128, 4, 128]), op=ALU.mult),
                        St_b + [epk_b], [Sdec_b], 512)
                if full:
                    for h in range(4):
                        stt(St[:, h * 128:(h + 1) * 128], sup[:, h * 128:(h + 1) * 128], epk[:, si, 8 + h:9 + h],
                            Sdec[:, h * 128:(h + 1) * 128], ALU.mult, ALU.add, [sup_b, epk_b, Sdec_b], [St_b[h]])
                else:
                    for h in range(4):
                        stt(St[:, h * 128:(h + 1) * 128], St[:, h * 128:(h + 1) * 128], epk[:, si, 4 + h:5 + h],
                            sup[:, h * 128:(h + 1) * 128], ALU.mult, ALU.add, [sup_b, epk_b, St_b[h]], [St_b[h]])
            if not full:
                return

            for h in range(4):
                act(ya_bf[0:TT, h * 128:(h + 1) * 128], obp[0:TT, h * 128:(h + 1) * 128], AF.Square, [obp_b],
                    [ost_b, ya_bf_b[h]], accum_out=ost[0:TT, h:h + 1])
            ts("pool", ost[0:TT, 4:8], ost[0:TT, 0:4], 1.0 / 128, EPS, ALU.mult, ALU.add, [ost_b], [ost_b])
            tt("pool", ost[0:TT, 8:12], ost[0:TT, 4:8], mhalf[0:TT, 0:1].to_broadcast([TT, 4]), ALU.pow,
               [ost_b, mhalf_b], [ost_b])
            for h in range(4):
                stt(ya_bf[0:TT, h * 128:(h + 1) * 128], obp[0:TT, h * 128:(h + 1) * 128], ost[0:TT, 8 + h:9 + h],
                    sga[0:TT, h * 128:(h + 1) * 128], ALU.mult, ALU.mult, [obp_b, ost_b, sga_b], [ya_bf_b[h]])
            tp, tp_b = next_tp(1)
            tr_group([(tp[:, h * 128:h * 128 + TT], ya_bf[0:TT, h * 128:(h + 1) * 128], ident_bf[0:TT, 0:TT])
                      for h in range(4)], ya_bf_b + [ident_bf_b], [tp_b])
            cp("dve", yT_bf[:, 0:4, 0:TT], tp[:, 0:512].rearrange("p (k t) -> p k t", k=4)[:, :, 0:TT],
               [tp_b], [yT_bf_b])

            vb, vb_b = proj_tm(C_V, Wi_blk[5])
            act(gv[0:TT, :], vb[0:TT, :], AF.Gelu, [vb_b], [gv_b])
            ub, ub_b = proj_fm(C_U, Wi_blk[4])
            act(gu[:, 0:W4], ub[:, 0:W4], AF.Gelu, [ub_b], [gu_b])
            veng("dve", lambda e: e.bn_stats(out=bnst[0:TT, 0:6], in_=gv[0:TT, :]), [gv_b], [bnst_b], 512)
            veng("dve", lambda e: e.bn_aggr(out=bnst[0:TT, 6:8], in_=bnst[0:TT, 0:6]), [bnst_b], [bnst_b])
            ts("pool", st2[0:TT, 4:5], bnst[0:TT, 7:8], EPS, None, ALU.add, None, [bnst_b], [st2_b])
            tt("pool", st2[0:TT, 5:6], st2[0:TT, 4:5], mhalf[0:TT, :], ALU.pow, [st2_b, mhalf_b], [st2_b])
            ts("dve", vhat_bf[0:TT, :], gv[0:TT, :], bnst[0:TT, 6:7], st2[0:TT, 5:6], ALU.subtract, ALU.mult,
               [gv_b, bnst_b, st2_b], [vhat_bf_b])
            if v_dst is not None:
                ts("dve", vn[0:TT, :], gv[0:TT, :], bnst[0:TT, 6:7], st2[0:TT, 5:6], ALU.subtract, ALU.mult,
                   [gv_b, bnst_b, st2_b], [vn_b])
                tt("pool", vn[0:TT, :], vn[0:TT, :], lnw_bc[0:TT, :], ALU.mult, [vn_b, lnw_bc_b], [vn_b])
                tt("pool", vn[0:TT, :], vn[0:TT, :], lnb_bc[0:TT, :], ALU.add, [vn_b, lnb_bc_b], [vn_b])
                dma(v_dst[0:16, :], vn[0:16, :], [vn_b], [], "vs", final=True)
                dma(v_dst[16:32, :], vn[32:48, :], [vn_b], [], "vs", final=True)
            items = [(sup[:, h * TT:(h + 1) * TT], vhat_bf[0:TT, h * 128:(h + 1) * 128], WsT_use[0:TT, h, 0:TT],
                      True, True) for h in range(4)]
            mm_group(items, [vhat_bf_b, WsT_bf_b, WsTs_bf_b], [sup_b])
            for h in range(4):
                stt(mixtmp[:, h * TT:(h + 1) * TT], sup[:, h * TT:(h + 1) * TT], lnwc(h), Bh_use[:, h, 0:TT],
                    ALU.mult, ALU.add, [sup_b, vT_b, Bh_b, Bhs_b], [mixtmp_b[h]])
            tt("pool", mixtmp[:, 0:W4], mixtmp[:, 0:W4], gu[:, 0:W4], ALU.mult, mixtmp_b + [gu_b], mixtmp_b)
            veng("dve", lambda e: e.tensor_tensor(
                out=yT_bf[:, 4:8, 0:TT], in0=mixtmp[:, 0:W4].rearrange("p (h t) -> p h t", h=4),
                in1=sgb[:, 0:W4].rearrange("p (h t) -> p h t", h=4), op=ALU.mult),
                mixtmp_b + [sgb_b], [yT_bf_b], W4)

            for half in range(2):
                opj, opj_b = (opj_own if OPJ_OWN[0] else next_ip())
                items = [(opj[0:TT, :], yT_bf[:, kt, 0:TT], Wo[:, kt, half * 512:(half + 1) * 512],
                          kt == 0, kt == KT - 1) for kt in range(KT)]
                mm_group(items, [yT_bf_b] + Wo_blk[half], [opj_b])
                tt("dve", hres[0:TT, half * 512:(half + 1) * 512], opj[0:TT, :], xt[0:TT, half * 512:(half + 1) * 512],
                   ALU.add, [opj_b, xt_b], [hres_b])
            act(xn_bf[0:TT, :], hres[0:TT, :], AF.Square, [hres_b], [st2_b, xn_bf_b], accum_out=st2[0:TT, 0:1])
            ts("pool", st2[0:TT, 1:2], st2[0:TT, 0:1], 1.0 / D, EPS, ALU.mult, ALU.add, [st2_b], [st2_b])
            tt("pool", st2[0:TT, 2:3], st2[0:TT, 1:2], mhalf[0:TT, :], ALU.pow, [st2_b, mhalf_b], [st2_b])
            stt(hres[0:TT, :], hres[0:TT, :], st2[0:TT, 2:3], fnw_bc[0:TT, :], ALU.mult, ALU.mult,
                [hres_b, st2_b, fnw_bc_b], [hres_b])
            if TT == 128:
                dma(out_dst, hres[:, :], [hres_b], [], "y_" + hres_b.name, final=True, nbytes=128 * D * 4)
            else:
                dma(out_dst[0:16, :], hres[0:16, :], [hres_b], [], "y_" + hres_b.name, final=True)
                dma(out_dst[16:32, :], hres[32:48, :], [hres_b], [], "y_" + hres_b.name, final=True)

        prompt_state = [(Sp, Sp_b, Smid_bf[0][0], Smid_bf[0][1])]
        R_xin_b = R_xin + [(wst[0][0][:, :, :].rearrange("p k c -> p (k c)"), wst[0][1]),
                           (wst[1][0][:, :, :].rearrange("p k c -> p (k c)"), wst[1][1])]
        RB = {"xin": R_xin_b, "xnT": R_xnT, "tf": R_tf, "logf": R_logf, "bcum": R_bcum, "enb": R_enb,
              "kkT": R_kkT, "kk": R_kk, "i": R_i, "Sdec": R_Sdec}
        if MODEL_DEEP[0] > 0:
            nc2 = bass.Bass("TRN2", target_bir_lowering=False)
            if MODEL_PSUM[0]:
                for j in range(4):
                    tpr.append((es.enter_context(nc2.psum_tensor(f"tpx{j}", [128, 1024], BF16)), Buf(f"tpx{j}")))

            def extra(name, shape, dt, k):
                return [(es.enter_context(nc2.sbuf_tensor(f"{name}_x{j}", list(shape), dt)), Buf(f"{name}_x{j}")) for j in range(k)]
            k_ = MODEL_DEEP[0]
            R_hres = R_hres + extra("xin", [128, D], F32, k_)
            R_yT = R_yT + extra("xnT", [128, KT, 128], BF16, k_)
            R_sq = R_sq + extra("tf", [128, 512], F32, k_)
            R_ebq = R_ebq + extra("logf", [128, 512], F32, k_)
            R_sga = R_sga + extra("bcum", [128, 512], F32, k_)
            R_gu = R_gu + extra("enb", [128, 512], F32, k_)
            R_qT = R_qT + extra("kkT", [128, 512], BF16, k_)
            R_scT = R_scT + extra("kk", [128, 512], BF16, k_)
            R_vhat = R_vhat + extra("i", [128, 512], BF16, k_)
            R_xn.extend(extra("xn", [128, D], BF16, k_))
            R_st1.extend(extra("st1", [128, 8], F32, k_))
            R_pk.extend(extra("pk", [128, 2, 16], F32, k_))
            R_epk.extend(extra("epk", [128, 2, 16], F32, k_))
        RA = {"xin": R_xin + R_hres, "xnT": R_xnT + R_yT, "tf": R_tf + R_sq, "logf": R_logf + R_ebq,
              "bcum": R_bcum + R_sga, "enb": R_enb + R_gu, "kkT": R_kkT + R_qT, "kk": R_kk + R_scT,
              "i": R_i + R_vhat, "Sdec": R_Sdec + R_sgb}

        for blk in (0, 3, 6, 5, 4):
            load_w_block(win_d, Wi, Wi_blk[blk], blk * 512, 512, nwc, [vT_b], defer=True)
        for half in range(2):
            load_w_block(wout_d, Wo, Wo_blk[half], half * 512, 512, lambda kt: (gnw(kt) if kt < 4 else None), [vT_b],
                         defer=True)
        if do_phase_a:
            ip_pool[0] = ip + [(scp, scp_b), opj_own]
            for t in range(n_pre):
                tile_pass(xpre_d[t * 128:(t + 1) * 128, :], 128, [(0, 128)], prompt_state, False,
                          None, None, None, None, RA)
                if wchunks:
                    load_w_chunk(*wchunks.pop(0))
        while wchunks:
            load_w_chunk(*wchunks.pop(0))
        ip_pool[0] = ip

        for t in range(n_tiles):
            tile_pass(xp_d[t * 128:(t + 1) * 128, :], 128, [(0, 128)], prompt_state, True,
                      WsT_bf, Bh, yp_d[t * 128:(t + 1) * 128, :], None, RB)
        veng("dve", lambda e: e.tensor_tensor(
            out=sout[:, :].rearrange("p (h v) -> p h v", h=4),
            in0=Sp[:, :].rearrange("p (h v) -> p h v", h=4),
            in1=lbc[:, 0:4].unsqueeze(2).to_broadcast([128, 4, 128]), op=ALU.mult),
            Sp_b + [lbc_b], [sout_b], 512)
        dma(sp_d.rearrange("h k v -> k h v"), sout[:, :].rearrange("p (h v) -> p h v", h=4), [sout_b], [], "spo",
            final=True, nbytes=262144)

        if do_sample:
            sst = []
            for q_ in range(2):
                St, St_b = Ss[q_]
                dma(St[:, :].rearrange("p (h v) -> p h v", h=4), s0_d[q_].rearrange("h k v -> k h v"), [], St_b,
                    f"s0{q_}", nbytes=262144)
                veng("dve", lambda e, St=St: e.tensor_tensor(
                    out=St[:, :].rearrange("p (h v) -> p h v", h=4),
                    in0=St[:, :].rearrange("p (h v) -> p h v", h=4),
                    in1=lbc[:, 16:20].unsqueeze(2).to_broadcast([128, 4, 128]), op=ALU.mult),
                    St_b + [lbc_b], St_b, 512)
                sst.append((St, St_b, Smid_bf[q_][0], Smid_bf[q_][1]))
            tile_pass(xs_d, 48, [(0, 16), (32, 16)], sst, True, WsTs_bf, Bhs, ys_d, vs_d, RB)
            for q_ in range(2):
                St, St_b = Ss[q_]
                veng("dve", lambda e, St=St: e.tensor_tensor(
                    out=sout[:, :].rearrange("p (h v) -> p h v", h=4),
                    in0=St[:, :].rearrange("p (h v) -> p h v", h=4),
                    in1=lbc[:, 0:4].unsqueeze(2).to_broadcast([128, 4, 128]), op=ALU.mult),
                    St_b + [lbc_b], [sout_b], 512)
                dma(ss_d[q_].rearrange("h k v -> k h v"), sout[:, :].rearrange("p (h v) -> p h v", h=4),
                    [sout_b], [], "sso", final=True, nbytes=262144)

        build_program.sbuf_free = nc.sbuf_bytes_remaining
        S.schedule(window=window)
        build_program.last_sched = S
        if MODEL_DEEP[0] > 0:
            return None
        sems = {k: es.enter_context(nc.semaphore("s_" + k)) for k in S.sem_keys()}
        with nc.allow_non_contiguous_dma(reason="small strided constant / state layouts"):
            with nc.Block() as block:
                @block.sync
                def _(e):
                    S.emit("sp", e, sems)

                @block.tensor
                def _(e):
                    S.emit("pe", e, sems)

                @block.scalar
                def _(e):
                    S.emit("act", e, sems)

                @block.vector
                def _(e):
                    S.emit("dve", e, sems)

                @block.gpsimd
                def _(e):
                    S.emit("pool", e, sems)
    build_program.last_sched = S
    return nc


_NC_CACHE = {}


def _get_nc():
    if "nc" not in _NC_CACHE:
        _NC_CACHE["nc"] = build_program()
    return _NC_CACHE["nc"]


def make_in_maps(x_prompt, x_sample, state_hgrn, norm_w, w_in, lb_logits, g_norm_w, ln_v_w, ln_v_b,
                 w_s, b_s, w_out, final_norm_w):
    f = lambda a: np.ascontiguousarray(np.asarray(a, dtype=np.float32))
    x_prompt, x_sample, state_hgrn = f(x_prompt), f(x_sample), f(state_hgrn)
    vecs = np.concatenate([
        f(lb_logits).reshape(8, 128), f(g_norm_w).reshape(4, 128), f(ln_v_w).reshape(4, 128),
        f(ln_v_b).reshape(4, 128), f(norm_w).reshape(8, 128)], axis=0)
    shared = {
        "vecs": f(vecs),
        "fnw": f(final_norm_w).reshape(1, D),
        "lnwrow": f(ln_v_w).reshape(1, 512),
        "lnbrow": f(ln_v_b).reshape(1, 512),
        "bsrow": f(b_s).reshape(1, 512),
        "ws": f(w_s).reshape(4, 128, 128),
        "win": f(w_in).reshape(D, INW),
        "wout": f(w_out).reshape(D, D),
    }
    in_maps = []
    for c in range(NCORES):
        b, s = c // 4, c % 4
        m = dict(shared)
        m["xp"] = f(x_prompt[b, s * SEG:(s + 1) * SEG, :])
        pre = np.zeros((3 * SEG, D), np.float32)
        if s > 0:
            pre[(3 - s) * SEG:, :] = x_prompt[b, 0:s * SEG, :]
        m["xpre"] = pre
        m["xs"] = f(x_sample[2 * c:2 * c + 2].reshape(32, D))
        m["s0"] = f(state_hgrn[0, 2 * c:2 * c + 2])
        in_maps.append(m)
    return in_maps


def kernel(x_prompt, x_sample, state_hgrn, norm_w, w_in, lb_logits, g_norm_w, ln_v_w, ln_v_b,
           w_s, b_s, w_out, final_norm_w):
    in_maps = make_in_maps(x_prompt, x_sample, state_hgrn, norm_w, w_in, lb_logits, g_norm_w, ln_v_w, ln_v_b,
                           w_s, b_s, w_out, final_norm_w)
    nc = _get_nc()
    res = run_bass_kernel_spmd(nc, in_maps, core_ids=list(range(NCORES)))
    r = res.results
    y_prompt = np.zeros((2, 8192, D), np.float32)
    y_sample = np.zeros((16, 16, D), np.float32)
    sp = np.zeros((1, 2, 4, 128, 128), np.float32)
    ss = np.zeros((1, 16, 4, 128, 128), np.float32)
    vs = np.zeros((1, 16, 16, 512), np.float32)
    for c in range(NCORES):
        b, s = c // 4, c % 4
        y_prompt[b, s * SEG:(s + 1) * SEG, :] = r[c]["yp"]
        y_sample[2 * c:2 * c + 2] = r[c]["ys"].reshape(2, 16, D)
        ss[0, 2 * c:2 * c + 2] = r[c]["sso"]
        vs[0, 2 * c:2 * c + 2] = r[c]["vso"].reshape(2, 16, 512)
        if s == 3:
            sp[0, b] = r[c]["spo"]
    return (y_prompt, y_sample, sp, ss, vs)
```
